# Optimizing a Trainium2 kernel written in Bass

```python
import math
import jax
import jax.numpy as jnp
from jax import lax
import numpy as np

D_MODEL = 2048
BATCH = 16
SEQ = 2048
DEPTH = 4
DEC_BATCH = 8
DEC_SEQ = 64
PAST_LEN = 2048

CHUNK = 64
N_MIXERS = 4
N_A = (DEPTH + 3) // N_MIXERS
N_B = (DEPTH + 2) // N_MIXERS
N_C = (DEPTH + 1) // N_MIXERS
N_D = DEPTH // N_MIXERS
EPS = 1e-6

CONV_A_WIDTH = 31
DN_HEAD_DIM = 128
DN_QK_HEADS = D_MODEL // DN_HEAD_DIM
DN_V_HEADS = 2 * DN_QK_HEADS
DN_QK_DIM = DN_QK_HEADS * DN_HEAD_DIM
DN_V_DIM = DN_V_HEADS * DN_HEAD_DIM
DN_QKV_DIM = 2 * DN_QK_DIM + DN_V_DIM
DN_IN_DIM = DN_QKV_DIM + DN_V_DIM + 2 * DN_V_HEADS
DN_CONV_WIDTH = 4
SC_WIDTH = 3
SWA_HEAD_DIM = 64
SWA_HEADS = D_MODEL // SWA_HEAD_DIM
SWA_KV_HEADS = 8
SWA_GROUP = SWA_HEADS // SWA_KV_HEADS
SWA_Q_DIM = SWA_HEADS * SWA_HEAD_DIM
SWA_KV_DIM = SWA_KV_HEADS * SWA_HEAD_DIM
WINDOW = 128
REL_BUCKETS = 32
REL_MAX_DIST = 128
FFN_HIDDEN = ((8 * D_MODEL // 3 + 255) // 256) * 256

kernel_name = "hybrid_chunk_causal_encoder_step"


def rmsnorm(x, g):
    xf = x.astype(jnp.float32)
    y = xf * lax.rsqrt(jnp.mean(xf * xf, axis=-1, keepdims=True) + EPS)
    return (y * g.astype(jnp.float32)).astype(x.dtype)


def layernorm(x, g, b):
    xf = x.astype(jnp.float32)
    xc = xf - jnp.mean(xf, axis=-1, keepdims=True)
    y = xc * lax.rsqrt(jnp.mean(xc * xc, axis=-1, keepdims=True) + EPS)
    return (y * g.astype(jnp.float32) + b.astype(jnp.float32)).astype(x.dtype)


def l2norm(x):
    return x * lax.rsqrt(jnp.sum(x * x, axis=-1, keepdims=True) + EPS)


def causal_dwconv(ext, w):
    return lax.conv_general_dilated(
        ext, w[:, None, :].astype(ext.dtype), window_strides=(1,), padding="VALID",
        dimension_numbers=("NWC", "WIO", "NWC"), feature_group_count=ext.shape[-1])


def swiglu(h, wg, wu, wd):
    return (jax.nn.silu(h @ wg) * (h @ wu)) @ wd


def conformer_conv(h, buf, w1, b1, dw, dwb, ln_g, ln_b, w2, b2):
    u = h @ w1 + b1
    u = u[..., :D_MODEL] * jax.nn.sigmoid(u[..., D_MODEL:])
    ext = jnp.concatenate([buf.astype(u.dtype), u], axis=1)
    c = layernorm(causal_dwconv(ext, dw) + dwb, ln_g, ln_b)
    return jax.nn.silu(c) @ w2 + b2, ext[:, -(CONV_A_WIDTH - 1):]


def gated_delta_rule(q, k, v, g, beta, s0, chunk):
    bsz, t, nh, _ = q.shape
    dv = v.shape[-1]
    n = t // chunk

    def blocks(a):
        a = a.reshape((bsz, n, chunk, nh) + a.shape[3:])
        return jnp.swapaxes(jnp.moveaxis(a, 1, 0), 2, 3)

    qb, kb, vb, gb, bb = blocks(q), blocks(k), blocks(v), blocks(g), blocks(beta)
    gc = jnp.cumsum(gb, axis=-1)
    pos = jnp.arange(chunk)
    incl = pos[:, None] >= pos[None, :]
    strict = pos[:, None] > pos[None, :]
    decay = jnp.exp(jnp.where(incl, gc[..., :, None] - gc[..., None, :], -jnp.inf))
    m = jnp.where(strict, bb[..., :, None] * jnp.einsum("nbhid,nbhjd->nbhij", kb, kb) * decay, 0.0)
    a = m + jnp.eye(chunk, dtype=m.dtype)
    rhs = jnp.concatenate([bb[..., None] * vb, (bb * jnp.exp(gc))[..., None] * kb], axis=-1)
    sol = lax.linalg.triangular_solve(a, rhs, left_side=True, lower=True, unit_diagonal=True)
    u, w = sol[..., :dv], sol[..., dv:]
    qk = jnp.where(incl, jnp.einsum("nbhid,nbhjd->nbhij", qb, kb) * decay, 0.0)
    qg = qb * jnp.exp(gc)[..., None]
    kg = kb * jnp.exp(gc[..., -1:] - gc)[..., None]
    glast = jnp.exp(gc[..., -1])

    def step(s, xs):
        u_c, w_c, qk_c, qg_c, kg_c, gl_c = xs
        v_new = u_c - jnp.einsum("bhik,bhkv->bhiv", w_c, s)
        o = jnp.einsum("bhik,bhkv->bhiv", qg_c, s) + jnp.einsum("bhij,bhjv->bhiv", qk_c, v_new)
        s = s * gl_c[..., None, None] + jnp.einsum("bhik,bhiv->bhkv", kg_c, v_new)
        return s, o

    s_fin, o = lax.scan(step, s0, (u, w, qk, qg, kg, glast))
    o = jnp.moveaxis(jnp.swapaxes(o, 2, 3), 0, 1).reshape(bsz, t, nh, dv)
    return o, s_fin


def gated_deltanet(h, s0, buf, w_in, conv_w, a_log, dt_bias, norm_g, w_out, chunk):
    bsz, t, _ = h.shape
    p = h @ w_in
    qkv = p[..., :DN_QKV_DIM]
    z = p[..., DN_QKV_DIM:DN_QKV_DIM + DN_V_DIM]
    b = p[..., DN_QKV_DIM + DN_V_DIM:DN_QKV_DIM + DN_V_DIM + DN_V_HEADS]
    a = p[..., DN_QKV_DIM + DN_V_DIM + DN_V_HEADS:]
    ext = jnp.concatenate([buf.astype(qkv.dtype), qkv], axis=1)
    c = jax.nn.silu(causal_dwconv(ext, conv_w)).astype(jnp.float32)
    q = c[..., :DN_QK_DIM].reshape(bsz, t, DN_QK_HEADS, DN_HEAD_DIM)
    k = c[..., DN_QK_DIM:2 * DN_QK_DIM].reshape(bsz, t, DN_QK_HEADS, DN_HEAD_DIM)
    v = c[..., 2 * DN_QK_DIM:].reshape(bsz, t, DN_V_HEADS, DN_HEAD_DIM)
    rep = DN_V_HEADS // DN_QK_HEADS
    q = jnp.repeat(l2norm(q), rep, axis=2) * (DN_HEAD_DIM ** -0.5)
    k = jnp.repeat(l2norm(k), rep, axis=2)
    beta = jax.nn.sigmoid(b.astype(jnp.float32))
    g = -jnp.exp(a_log.astype(jnp.float32)) * jax.nn.softplus(a.astype(jnp.float32) + dt_bias.astype(jnp.float32))
    o, s_new = gated_delta_rule(q, k, v, g, beta, s0.astype(jnp.float32), chunk)
    zf = z.astype(jnp.float32).reshape(bsz, t, DN_V_HEADS, DN_HEAD_DIM)
    o = o * lax.rsqrt(jnp.mean(o * o, axis=-1, keepdims=True) + EPS) * norm_g.astype(jnp.float32) * jax.nn.silu(zf)
    out = o.reshape(bsz, t, DN_V_DIM).astype(h.dtype) @ w_out
    return out, s_new, ext[:, -(DN_CONV_WIDTH - 1):]


def short_conv(h, buf, w_in, conv_w, w_out):
    p = h @ w_in
    bg, cg, xin = p[..., :D_MODEL], p[..., D_MODEL:2 * D_MODEL], p[..., 2 * D_MODEL:]
    ext = jnp.concatenate([buf.astype(p.dtype), cg * xin], axis=1)
    return (bg * causal_dwconv(ext, conv_w)) @ w_out, ext[:, -(SC_WIDTH - 1):]


def t5_bucket(rel):
    half = REL_BUCKETS // 2
    max_exact = half // 2
    a = jnp.abs(rel)
    af = jnp.maximum(a, 1).astype(jnp.float32)
    large = max_exact + (jnp.log(af / max_exact) / math.log(REL_MAX_DIST / max_exact)
                         * (half - max_exact)).astype(jnp.int32)
    large = jnp.minimum(large, half - 1)
    return jnp.where(rel > 0, half, 0) + jnp.where(a < max_exact, a, large)


def t5_bias(rel_bias, n_q, n_k):
    rel = jnp.arange(n_k)[None, :] - WINDOW - jnp.arange(n_q)[:, None]
    bias = jnp.take(rel_bias, t5_bucket(rel), axis=0).astype(jnp.float32)
    return jnp.transpose(bias, (2, 0, 1)).reshape(SWA_KV_HEADS, SWA_GROUP, n_q, n_k)


def sink_attention(q, k, v, bias, mask, sinks):
    s = jnp.einsum("...qngd,...knd->...ngqk", q, k).astype(jnp.float32) * (SWA_HEAD_DIM ** -0.5) + bias
    if mask is not None:
        s = jnp.where(mask, s, -jnp.inf)
    sink = sinks.astype(jnp.float32).reshape(SWA_KV_HEADS, SWA_GROUP, 1, 1)
    mx = jnp.maximum(jnp.max(s, axis=-1, keepdims=True), sink)
    p = jnp.exp(s - mx)
    denom = jnp.sum(p, axis=-1, keepdims=True) + jnp.exp(sink - mx)
    return jnp.einsum("...ngqk,...knd->...qngd", (p / denom).astype(v.dtype), v)


def swa_mixer(h, cache_k, cache_v, w_qkv, sinks, w_out, rel_bias):
    bsz, t, _ = h.shape
    qkv = h @ w_qkv
    q = qkv[..., :SWA_Q_DIM].reshape(bsz, t, SWA_KV_HEADS, SWA_GROUP, SWA_HEAD_DIM)
    k = qkv[..., SWA_Q_DIM:SWA_Q_DIM + SWA_KV_DIM].reshape(bsz, t, SWA_KV_HEADS, SWA_HEAD_DIM)
    v = qkv[..., SWA_Q_DIM + SWA_KV_DIM:].reshape(bsz, t, SWA_KV_HEADS, SWA_HEAD_DIM)
    if cache_k is None:
        nc, nw = t // CHUNK, WINDOW // CHUNK
        pad = jnp.zeros((bsz, WINDOW, SWA_KV_HEADS, SWA_HEAD_DIM), k.dtype)
        k_ext = jnp.concatenate([pad, k], axis=1)
        v_ext = jnp.concatenate([pad, v], axis=1)
        kp = k_ext.reshape(bsz, nc + nw, CHUNK, SWA_KV_HEADS, SWA_HEAD_DIM)
        vp = v_ext.reshape(bsz, nc + nw, CHUNK, SWA_KV_HEADS, SWA_HEAD_DIM)
        kb = jnp.concatenate([kp[:, j:j + nc] for j in range(nw + 1)], axis=2)
        vb = jnp.concatenate([vp[:, j:j + nc] for j in range(nw + 1)], axis=2)
        qb = q.reshape(bsz, nc, CHUNK, SWA_KV_HEADS, SWA_GROUP, SWA_HEAD_DIM)
        key_pos = jnp.arange(nc)[:, None] * CHUNK - WINDOW + jnp.arange(WINDOW + CHUNK)[None, :]
        mask = (key_pos >= 0)[:, None, None, None, :]
        o = sink_attention(qb, kb, vb, t5_bias(rel_bias, CHUNK, WINDOW + CHUNK), mask, sinks)
    else:
        k_ext = jnp.concatenate([cache_k.astype(k.dtype), k], axis=1)
        v_ext = jnp.concatenate([cache_v.astype(v.dtype), v], axis=1)
        o = sink_attention(q, k_ext, v_ext, t5_bias(rel_bias, t, WINDOW + t), None, sinks)
    o = o.reshape(bsz, t, SWA_Q_DIM)
    return o @ w_out, k_ext[:, -WINDOW:], v_ext[:, -WINDOW:]


def run_group(x, conv_a, delta_s, delta_conv, sconv, swa_k, swa_v, w, first_chunk):
    chunk = CHUNK if first_chunk else x.shape[1]
    new_conv_a, new_ds, new_dc, new_sc, new_k, new_v = [], [], [], [], [], []
    for i in range(DEPTH):
        mix, j = i % N_MIXERS, i // N_MIXERS
        h = rmsnorm(x, w["norm_mix_pre"][i])
        if mix == 0:
            out, buf = conformer_conv(h, conv_a[j], w["conv_a_w1"][j], w["conv_a_b1"][j], w["conv_a_dw"][j],
                                      w["conv_a_dw_b"][j], w["conv_a_ln_g"][j], w["conv_a_ln_b"][j],
                                      w["conv_a_w2"][j], w["conv_a_b2"][j])
            new_conv_a.append(buf)
        elif mix == 1:
            out, s_new, buf = gated_deltanet(h, delta_s[j], delta_conv[j], w["delta_w_in"][j], w["delta_conv_w"][j],
                                             w["delta_a_log"][j], w["delta_dt_bias"][j], w["delta_norm_g"][j],
                                             w["delta_w_out"][j], chunk)
            new_ds.append(s_new)
            new_dc.append(buf)
        elif mix == 2:
            out, buf = short_conv(h, sconv[j], w["sconv_w_in"][j], w["sconv_w"][j], w["sconv_w_out"][j])
            new_sc.append(buf)
        else:
            ck = None if first_chunk else swa_k[j]
            cv = None if first_chunk else swa_v[j]
            out, nk, nv = swa_mixer(h, ck, cv, w["swa_w_qkv"][j], w["swa_sinks"][j], w["swa_w_out"][j], w["rel_bias"])
            new_k.append(nk)
            new_v.append(nv)
        x = x + rmsnorm(out, w["norm_mix_post"][i])
        h = rmsnorm(x, w["norm_ffn_pre"][i])
        x = x + rmsnorm(swiglu(h, w["ffn_w_gate"][i], w["ffn_w_up"][i], w["ffn_w_down"][i]), w["norm_ffn_post"][i])
    return x, (jnp.stack(new_conv_a), jnp.stack(new_ds), jnp.stack(new_dc),
               jnp.stack(new_sc), jnp.stack(new_k), jnp.stack(new_v))


def setup_inputs(seed: int = 0) -> dict:
    key = jax.random.key(seed)
    ks = iter(jax.random.split(key, 48))

    def nrm(shape, scale=1.0):
        return jax.random.normal(next(ks), shape, jnp.float32) * scale

    def gain(shape):
        return 1.0 + nrm(shape, 0.05)

    d, f = D_MODEL, FFN_HIDDEN
    return {
        "x_prompt": nrm((BATCH, SEQ, d)),
        "x_sample": nrm((DEC_BATCH, DEC_SEQ, d)),
        "cache_conv_a": nrm((N_A, DEC_BATCH, CONV_A_WIDTH - 1, d)),
        "state_delta_s": nrm((N_B, DEC_BATCH, DN_V_HEADS, DN_HEAD_DIM, DN_HEAD_DIM), 0.05),
        "state_delta_conv": nrm((N_B, DEC_BATCH, DN_CONV_WIDTH - 1, DN_QKV_DIM)),
        "cache_sconv": nrm((N_C, DEC_BATCH, SC_WIDTH - 1, d)),
        "cache_swa_k": nrm((N_D, DEC_BATCH, WINDOW, SWA_KV_HEADS, SWA_HEAD_DIM)),
        "cache_swa_v": nrm((N_D, DEC_BATCH, WINDOW, SWA_KV_HEADS, SWA_HEAD_DIM)),
        "rel_bias": nrm((REL_BUCKETS, SWA_HEADS), 0.5),
        "norm_mix_pre": gain((DEPTH, d)),
        "norm_mix_post": gain((DEPTH, d)),
        "norm_ffn_pre": gain((DEPTH, d)),
        "norm_ffn_post": gain((DEPTH, d)),
        "ffn_w_gate": nrm((DEPTH, d, f), d ** -0.5),
        "ffn_w_up": nrm((DEPTH, d, f), d ** -0.5),
        "ffn_w_down": nrm((DEPTH, f, d), f ** -0.5),
        "conv_a_w1": nrm((N_A, d, 2 * d), d ** -0.5),
        "conv_a_b1": nrm((N_A, 2 * d), 0.02),
        "conv_a_dw": nrm((N_A, CONV_A_WIDTH, d), CONV_A_WIDTH ** -0.5),
        "conv_a_dw_b": nrm((N_A, d), 0.02),
        "conv_a_ln_g": gain((N_A, d)),
        "conv_a_ln_b": nrm((N_A, d), 0.02),
        "conv_a_w2": nrm((N_A, d, d), d ** -0.5),
        "conv_a_b2": nrm((N_A, d), 0.02),
        "delta_w_in": nrm((N_B, d, DN_IN_DIM), d ** -0.5),
        "delta_conv_w": nrm((N_B, DN_CONV_WIDTH, DN_QKV_DIM), DN_CONV_WIDTH ** -0.5),
        "delta_a_log": jnp.log(jax.random.uniform(next(ks), (N_B, DN_V_HEADS), jnp.float32, 1.0, 8.0)),
        "delta_dt_bias": jax.random.uniform(next(ks), (N_B, DN_V_HEADS), jnp.float32, -4.6, -2.3),
        "delta_norm_g": gain((N_B, DN_HEAD_DIM)),
        "delta_w_out": nrm((N_B, DN_V_DIM, d), DN_V_DIM ** -0.5),
        "sconv_w_in": nrm((N_C, d, 3 * d), d ** -0.5),
        "sconv_w": nrm((N_C, SC_WIDTH, d), SC_WIDTH ** -0.5),
        "sconv_w_out": nrm((N_C, d, d), d ** -0.5),
        "swa_w_qkv": nrm((N_D, d, SWA_Q_DIM + 2 * SWA_KV_DIM), d ** -0.5),
        "swa_sinks": nrm((N_D, SWA_HEADS), 0.5),
        "swa_w_out": nrm((N_D, SWA_Q_DIM, d), SWA_Q_DIM ** -0.5),
    }


def reference(x_prompt, x_sample, cache_conv_a, state_delta_s, state_delta_conv, cache_sconv,
              cache_swa_k, cache_swa_v, rel_bias, norm_mix_pre, norm_mix_post, norm_ffn_pre,
              norm_ffn_post, ffn_w_gate, ffn_w_up, ffn_w_down, conv_a_w1, conv_a_b1, conv_a_dw,
              conv_a_dw_b, conv_a_ln_g, conv_a_ln_b, conv_a_w2, conv_a_b2, delta_w_in, delta_conv_w,
              delta_a_log, delta_dt_bias, delta_norm_g, delta_w_out, sconv_w_in, sconv_w, sconv_w_out,
              swa_w_qkv, swa_sinks, swa_w_out):
    w = {
        "rel_bias": rel_bias, "norm_mix_pre": norm_mix_pre, "norm_mix_post": norm_mix_post,
        "norm_ffn_pre": norm_ffn_pre, "norm_ffn_post": norm_ffn_post, "ffn_w_gate": ffn_w_gate,
        "ffn_w_up": ffn_w_up, "ffn_w_down": ffn_w_down, "conv_a_w1": conv_a_w1, "conv_a_b1": conv_a_b1,
        "conv_a_dw": conv_a_dw, "conv_a_dw_b": conv_a_dw_b, "conv_a_ln_g": conv_a_ln_g,
        "conv_a_ln_b": conv_a_ln_b, "conv_a_w2": conv_a_w2, "conv_a_b2": conv_a_b2,
        "delta_w_in": delta_w_in, "delta_conv_w": delta_conv_w, "delta_a_log": delta_a_log,
        "delta_dt_bias": delta_dt_bias, "delta_norm_g": delta_norm_g, "delta_w_out": delta_w_out,
        "sconv_w_in": sconv_w_in, "sconv_w": sconv_w, "sconv_w_out": sconv_w_out,
        "swa_w_qkv": swa_w_qkv, "swa_sinks": swa_sinks, "swa_w_out": swa_w_out,
    }
    bsz = x_prompt.shape[0]
    dt = x_prompt.dtype
    conv_a0 = jnp.zeros((N_A, bsz, CONV_A_WIDTH - 1, D_MODEL), dt)
    delta_s0 = jnp.zeros((N_B, bsz, DN_V_HEADS, DN_HEAD_DIM, DN_HEAD_DIM), jnp.float32)
    delta_conv0 = jnp.zeros((N_B, bsz, DN_CONV_WIDTH - 1, DN_QKV_DIM), dt)
    sconv0 = jnp.zeros((N_C, bsz, SC_WIDTH - 1, D_MODEL), dt)
    y_prompt, (ca_p, ds_p, dc_p, sc_p, k_p, v_p) = run_group(
        x_prompt, conv_a0, delta_s0, delta_conv0, sconv0, None, None, w, True)
    y_sample, (ca_s, ds_s, dc_s, sc_s, k_s, v_s) = run_group(
        x_sample, cache_conv_a, state_delta_s, state_delta_conv, cache_sconv, cache_swa_k, cache_swa_v, w, False)
    return (y_prompt, y_sample, ca_p, ca_s, ds_p, ds_s, dc_p, dc_s, sc_p, sc_s, k_p, k_s, v_p, v_s)
```

```python
import contextlib
import numpy as np
import concourse.bass as bass
import concourse.mybir as mybir
from concourse.bass_utils import run_bass_kernel_spmd

F32 = mybir.dt.float32
BF16 = mybir.dt.bfloat16
AF = mybir.ActivationFunctionType
ALU = mybir.AluOpType
AX = mybir.AxisListType

D = 2048
NCH = 16
FF = 5632
FCH = 44
EPS = 1e-6
SAME_ENGINE_SYNC = True

CFG = {"nlayers": 4, "seqs": ("p0", "p1", "s"), "ptiles": 4, "L": 64}


class Chan:
    def __init__(self, idx):
        self.idx = idx
        self.count = 0


class Prog:
    ENG = ("pe", "act", "dve", "pool", "sp")

    def __init__(self):
        self.q = {e: [] for e in self.ENG}
        self.cnt = {e: 0 for e in self.ENG}
        self.seen = {e: {} for e in self.ENG}
        self.lastw = {}
        self.readers = {}
        self.chans = []
        self.nops = 0

    def chan(self):
        c = Chan(len(self.chans))
        self.chans.append(c)
        return c

    def _need(self, e, tk, waits):
        semkey, val = tk
        if semkey == e and (e == "pe" or not SAME_ENGINE_SYNC):
            return
        if self.seen[e].get(semkey, 0) >= val:
            return
        self.seen[e][semkey] = val
        waits.append((semkey, val))

    def _deps(self, e, reads, writes):
        waits = []
        for k in reads:
            t = self.lastw.get(k)
            if t is not None:
                self._need(e, t, waits)
        for k in writes:
            t = self.lastw.get(k)
            if t is not None:
                self._need(e, t, waits)
            for t in self.readers.get(k, {}).values():
                self._need(e, t, waits)
        return waits

    def _commit(self, tk, reads, writes):
        for k in reads:
            self.readers.setdefault(k, {})[tk[0]] = tk
        for k in writes:
            self.lastw[k] = tk
            self.readers[k] = {}

    @staticmethod
    def _excl(reads, writes):
        r2, w2 = [], list(writes)
        for k in reads:
            if isinstance(k, tuple) and k[0] in ("PS", "PSQ"):
                w2.append(k)
            else:
                r2.append(k)
        w2 = [("PS", k[1]) if (isinstance(k, tuple) and k[0] == "PSQ") else k for k in w2]
        return r2, w2

    def op(self, e, meth, reads, writes, *a, **kw):
        reads, writes = self._excl(reads, writes)
        waits = self._deps(e, reads, writes)
        self.cnt[e] += 1
        tk = (e, self.cnt[e])
        self.q[e].append((waits, meth, a, kw, e, 1))
        self._commit(tk, reads, writes)
        self.nops += 1
        return tk

    def dma(self, e, ch, reads, writes, out, in_, **kw):
        waits = self._deps(e, reads, writes)
        if ch.count > 0:
            self._need(e, (("ch", ch.idx), 16 * ch.count), waits)
        ch.count += 1
        tk = (("ch", ch.idx), 16 * ch.count)
        kw = dict(kw)
        kw["out"] = out
        kw["in_"] = in_
        self.q[e].append((waits, "dma_start", (), kw, ("ch", ch.idx), 16))
        self._commit(tk, reads, writes)
        self.nops += 1
        return tk

    def dma_group(self, e, ch, reads, writes, pairs, **kw):
        waits = self._deps(e, reads, writes)
        if ch.count > 0:
            self._need(e, (("ch", ch.idx), 16 * ch.count), waits)
        for i, (out, in_) in enumerate(pairs):
            ch.count += 1
            k2 = dict(kw)
            k2["out"] = out
            k2["in_"] = in_
            self.q[e].append((waits if i == 0 else [], "dma_start", (), k2, ("ch", ch.idx), 16))
        tk = (("ch", ch.idx), 16 * ch.count)
        self._commit(tk, reads, writes)
        self.nops += len(pairs)
        return tk

    def barrier(self, chans=()):
        for e in self.ENG:
            waits = []
            for e2 in ("pe", "act", "dve", "pool"):
                if e2 != e and self.cnt[e2] > 0:
                    self._need(e, (e2, self.cnt[e2]), waits)
            for ch in chans:
                if ch.count > 0:
                    self._need(e, (("ch", ch.idx), 16 * ch.count), waits)
            if waits:
                self.q[e].append((waits, None, (), {}, None, 0))

    def final_waits(self, e):
        waits = []
        for ch in self.chans:
            if ch.count > 0:
                self._need(e, (("ch", ch.idx), 16 * ch.count), waits)
        for e2 in ("pe", "act", "dve", "pool"):
            if e2 != e and self.cnt[e2] > 0:
                self._need(e, (e2, self.cnt[e2]), waits)
        self.q[e].append((waits, None, (), {}, None, 0))

    def replay(self, e, eng, semh):
        for waits, meth, a, kw, inck, inca in self.q[e]:
            for semkey, val in waits:
                eng.wait_ge(semh[semkey], val)
            if meth is None:
                continue
            ins = getattr(eng, meth)(*a, **kw)
            ins.then_inc(semh[inck], inca)


WSPEC = [
    ("conv_a_w1", D, 2 * D), ("conv_a_w2", D, D),
    ("delta_w_in", D, 12352), ("delta_w_out", 2 * D, D),
    ("sconv_w_in", D, 3 * D), ("sconv_w_out", D, D),
    ("swa_w_qkv", D, 3072), ("swa_w_out", D, D),
]
FFN_W = [("ffn_w_gate", D, FF), ("ffn_w_up", D, FF), ("ffn_w_down", FF, D)]

SMALL = [
    ("rel_bias", [32, 32]), ("norm_mix_pre", [4, D]), ("norm_mix_post", [4, D]), ("norm_ffn_pre", [4, D]),
    ("norm_ffn_post", [4, D]), ("conv_a_b1", [1, 2 * D]), ("conv_a_dw", [1, 31, D]), ("conv_a_dw_b", [1, D]),
    ("conv_a_ln_g", [1, D]), ("conv_a_ln_b", [1, D]), ("conv_a_b2", [1, D]), ("delta_conv_w", [1, 4, 8192]),
    ("delta_a_log", [1, 32]), ("delta_dt_bias", [1, 32]), ("delta_norm_g", [1, 128]), ("sconv_w", [1, 3, D]),
    ("swa_sinks", [1, 32]),
]


def host_consts(L=64):
    c = np.zeros((128, 1024), np.float32)
    c[:, 0:128] = np.eye(128, dtype=np.float32)
    i = np.arange(128)
    blk = (i[:, None] // 64 == i[None, :] // 64)
    c[:, 128:256] = -((i[:, None] > i[None, :]) & blk).astype(np.float32)
    c[:, 256:384] = ((i[None, :] >= i[:, None]) & blk).astype(np.float32)
    r = np.ones((128, 512), np.float32)
    r[:, 0::L] = 0.0
    c[:, 384:896] = r
    return c


def host_onehot():
    q = np.arange(64)[:, None]
    k = np.arange(192)[None, :]
    rel = k - 128 - q
    half, max_exact = 16, 8
    a = np.abs(rel)
    af = np.maximum(a, 1).astype(np.float32)
    large = max_exact + (np.log(af / max_exact) / np.float32(np.log(128 / max_exact)) * (half - max_exact)).astype(np.int32)
    large = np.minimum(large, half - 1)
    buck = np.where(rel > 0, half, 0) + np.where(a < max_exact, a, large)
    oh = np.zeros((32, 64, 192), np.float32)
    for b in range(32):
        oh[b] = (buck == b)
    return oh.reshape(32, 64 * 192)


class K:
    def __init__(self, cfg):
        self.cfg = cfg
        self.P = Prog()
        self.nc = bass.Bass("TRN2", target_bir_lowering=False)
        self.psn = 0

    def decl(self):
        nc = self.nc
        di = lambda n, s: nc.dram_tensor(n, s, F32, kind="ExternalInput").ap()
        do = lambda n, s: nc.dram_tensor(n, s, F32, kind="ExternalOutput").ap()
        self.xp = di("x_prompt", [2, 2048, D])
        self.xs = di("x_sample", [64, D])
        self.c_conv_a = di("cache_conv_a", [30, D])
        self.c_ds = di("state_delta_s", [32, 128, 128])
        self.c_dc = di("state_delta_conv", [3, 8192])
        self.c_sc = di("cache_sconv", [2, D])
        self.c_k = di("cache_swa_k", [128, 512])
        self.c_v = di("cache_swa_v", [128, 512])
        self.cst = di("cst", [128, 1024])
        self.ohd = di("onehot", [32, 64 * 192])
        self.sm = {n: di(n, s) for n, s in SMALL}
        self.wf = {}
        self.wb = {}
        nl = self.cfg["nlayers"]
        for li, (n, k, m) in enumerate(WSPEC):
            if li // 2 >= nl:
                continue
            self.wf[n] = di(n, [k, m])
            self.wb[n] = nc.dram_tensor(n + "_bf", [k, m], BF16, kind="Internal").ap()
        for n, k, m in FFN_W:
            self.wf[n] = di(n, [nl, k, m])
            self.wb[n] = nc.dram_tensor(n + "_bf", [nl, k, m], BF16, kind="Internal").ap()
        self.bt_d = nc.dram_tensor("bias_scr", [32, 64 * 192], F32, kind="Internal").ap()
        self.o_y_p = do("y_prompt", [2, 2048, D])
        self.o_y_s = do("y_sample", [64, D])
        self.o_ca_p = do("conv_a_p", [2, 30, D])
        self.o_ca_s = do("conv_a_s", [30, D])
        self.o_ds_p = do("ds_p", [2, 32, 128, 128])
        self.o_ds_s = do("ds_s", [32, 128, 128])
        self.o_dc_p = do("dc_p", [2, 3, 8192])
        self.o_dc_s = do("dc_s", [3, 8192])
        self.o_sc_p = do("sc_p", [2, 2, D])
        self.o_sc_s = do("sc_s", [2, D])
        self.o_k_p = do("k_p", [2, 128, 512])
        self.o_k_s = do("k_s", [128, 512])
        self.o_v_p = do("v_p", [2, 128, 512])
        self.o_v_s = do("v_s", [128, 512])

    def alloc(self, es):
        nc = self.nc
        NW = 45056
        self.arena = es.enter_context(nc.sbuf_tensor("arena", [128, NW], F32))
        self.apos = 0
        self.ps = [es.enter_context(nc.psum_tensor(f"ps{i}", [128, 512], F32)) for i in range(8)]

        def take(nw):
            a = self.apos
            self.apos += nw
            assert self.apos <= NW, self.apos
            return self.arena[:, a:a + nw]
        self.take = take
        self.Xr = take(8192)
        self.Or = take(8192)
        self.Hr = take(4096)
        self.Gr = take(8704)
        self.Wr = [take(2048) for _ in range(2)]
        self.Sr = take(4096)
        self.CST = take(896)
        self.RS = take(512)
        self.T1 = take(512)
        self.T2 = take(512)
        self.g_mpre = take(64)
        self.g_mpost = take(64)
        self.g_fpre = take(64)
        self.g_fpost = take(64)
        self.ca_b1 = take(32)
        self.ca_dw = take(31 * 16)
        self.ca_dwb = take(16)
        self.ca_lng = take(16)
        self.ca_lnb = take(16)
        self.ca_b2 = take(16)
        self.EXTH = take(16 * 30)
        self.sc_w = take(3 * 16)
        self.SCH = take(2 * 16)
        self.onesb = take(64)
        self.identb = take(64)
        self.STG = take(512)
        self.KTH = take(512)
        self.VH = take(1024)
        self.SINK = take(16)
        self.DCH = take(192)
        self.DCW = take(256)
        self.dcol = take(4)
        self.onec = take(1)
        self.ones64 = take(128)
        self.ps_lo, self.ps_n = 0, 8
        self.wchan = [self.P.chan() for _ in range(2)]
        self.och = self.P.chan()
        self.ich = self.P.chan()
        self.mch = self.P.chan()

    def psum(self):
        i = self.ps_lo + self.psn % self.ps_n
        self.psn += 1
        return i

    def v3(self, reg, n, dt=F32, c=None):
        ap = reg if dt == F32 else reg.bitcast(BF16)
        tot = ap.shape[1]
        if c is not None:
            ap = ap[:, :c * n]
        else:
            ap = ap[:, :(tot // n) * n]
        return ap.rearrange("p (c n) -> p c n", n=n)

    def setup(self):
        P = self.P
        P.dma("sp", self.mch, [], ["CST"], self.CST, self.cst[:, 0:896])
        self.ident = self.CST[:, 0:128]
        self.maskLs = self.CST[:, 128:256]
        self.maskU = self.CST[:, 256:384]
        self.reset = self.CST[:, 384:896]
        P.op("pool", "memset", [], ["onesb"], self.onesb.bitcast(BF16), 1.0)
        P.op("dve", "tensor_copy", ["CST"], ["identb"], self.identb.bitcast(BF16), self.ident)
        self.wch = {}
        order = ["conv_a_w1", "conv_a_w2", ("ffn", 0), "delta_w_in", "delta_w_out", ("ffn", 1), "sconv_w_in",
                 "sconv_w_out", ("ffn", 2), "swa_w_qkv", "swa_w_out", ("ffn", 3)]
        used_layers = self.cfg["nlayers"]
        lay_of = {"conv_a_w1": 0, "conv_a_w2": 0, "delta_w_in": 1, "delta_w_out": 1, "sconv_w_in": 2,
                  "sconv_w_out": 2, "swa_w_qkv": 3, "swa_w_out": 3}

        def cast(key, src, dst):
            ch = self.P.chan()
            k, m = src.shape
            mb = m
            for cand in (2048, 1544, 1536, 1408, 1024):
                if m % cand == 0:
                    mb = cand
                    break
            pairs = []
            rb = 512
            for r0 in range(0, k, rb):
                r1 = min(k, r0 + rb)
                pairs.append((dst[r0:r1, :].rearrange("k (a b) -> k a b", b=mb),
                              src[r0:r1, :].rearrange("k (a b) -> k a b", b=mb)))
            P.dma_group("pool", ch, [], [key], pairs)
        for it in order:
            if isinstance(it, tuple):
                l = it[1]
                if l >= used_layers:
                    continue
                for n, _, _ in FFN_W:
                    cast(("WS", n, l), self.wf[n][l], self.wb[n][l])
            else:
                if lay_of[it] >= used_layers:
                    continue
                cast(("WS", it, 0), self.wf[it], self.wb[it])
        nc = self.nc
        with nc.allow_non_contiguous_dma("small param transposing loads"):
            pass
        kw = dict(allow_slow_non_contiguous=True)

        def ldcol(dst, src2d, key, nrow, ncol):
            d3 = dst.rearrange("p (r c) -> p r c", c=ncol)
            pairs = [(d3[:, r, :], src2d[r].rearrange("(c p) -> p c", p=128)) for r in range(nrow)]
            P.dma_group("sp", self.mch, [], [key], pairs, **kw)
        ldcol(self.g_mpre, self.sm["norm_mix_pre"], "g_mpre", 4, 16)
        ldcol(self.g_mpost, self.sm["norm_mix_post"], "g_mpost", 4, 16)
        ldcol(self.g_fpre, self.sm["norm_ffn_pre"], "g_fpre", 4, 16)
        ldcol(self.g_fpost, self.sm["norm_ffn_post"], "g_fpost", 4, 16)
        ldcol(self.ca_b1, self.sm["conv_a_b1"], "ca_b1", 1, 32)
        ldcol(self.ca_dw, self.sm["conv_a_dw"][0], "ca_dw", 31, 16)
        ldcol(self.ca_dwb, self.sm["conv_a_dw_b"], "ca_dwb", 1, 16)
        ldcol(self.ca_lng, self.sm["conv_a_ln_g"], "ca_lng", 1, 16)
        ldcol(self.ca_lnb, self.sm["conv_a_ln_b"], "ca_lnb", 1, 16)
        ldcol(self.ca_b2, self.sm["conv_a_b2"], "ca_b2", 1, 16)
        ldcol(self.sc_w, self.sm["sconv_w"][0], "sc_w", 3, 16)

    def gemm(self, groups, kch, in_ap, in_key, N, epi, tag):
        P = self.P
        maxcols = max(sum(sum(p[3] for p in ch) for ch in g) for g in groups)
        kpart = kch
        while kpart * maxcols > 4096:
            assert kpart % 2 == 0
            kpart //= 2
        nkp = kch // kpart
        slabs = [(gi, kp) for gi in range(len(groups)) for kp in range(nkp)]
        self._slabn = getattr(self, "_slabn", 0)

        def load(si):
            gi, kp = slabs[si]
            slot = (self._slabn + si) % 2
            g = groups[gi]
            cols = sum(sum(p[3] for p in ch) for ch in g)
            sv = self.Wr[slot].bitcast(BF16)[:, :kpart * cols].rearrange("p (k m) -> p k m", m=cols)
            pairs = []
            rk = set()
            off = 0
            for ch in g:
                for (wkey, wap, c0, wd) in ch:
                    pairs.append((sv[:, :, off:off + wd],
                                  wap[kp * kpart * 128:(kp + 1) * kpart * 128, c0:c0 + wd].rearrange("(k p) m -> p k m", p=128)))
                    rk.add(wkey)
                    off += wd
            P.dma_group("sp", self.wchan[slot], list(rk), [("W", slot)], pairs)
            return sv
        svs = {0: load(0)}
        held = None
        for si, (gi, kp) in enumerate(slabs):
            if si + 1 < len(slabs):
                svs[si + 1] = load(si + 1)
            sv = svs.pop(si)
            slot = (self._slabn + si) % 2
            g = groups[gi]
            if kp == 0:
                held = [(self.psum(), sum(p[3] for p in ch)) for ch in g]
            off = 0
            for ci, ch in enumerate(g):
                pi, M = held[ci]
                for k in range(kpart):
                    kk = kp * kpart + k
                    P.op("pe", "matmul", [("W", slot), in_key(kk)], [("PS", pi)],
                         self.ps[pi][:M, :N], sv[:, k, off:off + M], in_ap(kk),
                         start=(kk == 0), stop=(kk == kch - 1))
                off += M
            if kp == nkp - 1:
                epi(gi, held)
        self._slabn += len(slabs)

    def sumsq_rstd(self, src3, srckeys, sq3, sqkeys, N, nch, scale):
        P = self.P
        for c in range(nch):
            P.op("act", "activation", [srckeys[c]], [sqkeys[c]], sq3[:, c, :], src3[:, c, :], AF.Square)
        pi = self.psum()
        onesb = self.onesb.bitcast(BF16)
        for c in range(nch):
            P.op("pe", "matmul", ["onesb", sqkeys[c]], [("PS", pi)], self.ps[pi][:, :N], onesb, sq3[:, c, :],
                 start=(c == 0), stop=(c == nch - 1))
        P.op("act", "activation", [("PS", pi)], ["RS"], self.RS[:, :N], self.ps[pi][:, :N], AF.Sqrt,
             bias=self.epsc, scale=scale)
        P.op("dve", "reciprocal", ["RS"], ["RS"], self.RS[:, :N], self.RS[:, :N])

    def prenorm(self, N, gcol):
        P = self.P
        X3 = self.v3(self.Xr, N, c=16)
        H3 = self.v3(self.Hr, N, BF16, c=16)
        SQ = self.v3(self.Gr, N, BF16, c=16)
        self.sumsq_rstd(X3, [("X", c) for c in range(16)], SQ, [("G", c) for c in range(16)], N, 16, 1.0 / D)
        for c in range(16):
            P.op("dve", "scalar_tensor_tensor", [("X", c), "RS"], [("H", c)], H3[:, c, :], X3[:, c, :],
                 gcol[:, c:c + 1], self.RS[:, :N], op0=ALU.mult, op1=ALU.mult)

    def postnorm_add(self, N, gcol):
        P = self.P
        X3 = self.v3(self.Xr, N, c=16)
        O3 = self.v3(self.Or, N, c=16)
        SQ = self.v3(self.Hr, N, BF16, c=16)
        self.sumsq_rstd(O3, [("O", c) for c in range(16)], SQ, [("H", c) for c in range(16)], N, 16, 1.0 / D)
        for c in range(16):
            P.op("dve", "scalar_tensor_tensor", [("O", c), "RS"], [("O", c)], O3[:, c, :], O3[:, c, :],
                 gcol[:, c:c + 1], self.RS[:, :N], op0=ALU.mult, op1=ALU.mult)
            P.op("dve", "tensor_tensor", [("O", c), ("X", c)], [("X", c)], X3[:, c, :], X3[:, c, :], O3[:, c, :],
                 op=ALU.add)

    def load_x(self, src2d, N):
        P = self.P
        nb = (N + 127) // 128
        pr = min(N, 128)
        Ot = self.Or[:, :nb * D].rearrange("p (b d) -> p b d", d=D)
        okeys = [("O", c) for c in range(16)]
        P.dma("sp", self.ich, [], okeys, Ot[:pr, :, :], src2d.rearrange("(b p) d -> p b d", p=pr))
        X3 = self.v3(self.Xr, N, c=16)
        for c in range(16):
            pi = self.psum()
            for b in range(nb):
                P.op("pe", "transpose", okeys + ["CST"], [("PS", pi)], self.ps[pi][:, b * 128:b * 128 + pr],
                     Ot[:pr, b, c * 128:(c + 1) * 128], self.ident[:pr, :pr])
            e = "act" if c % 2 else "dve"
            if e == "act":
                P.op("act", "activation", [("PS", pi)], [("X", c)], X3[:, c, :], self.ps[pi][:, :N], AF.Copy)
            else:
                P.op("dve", "tensor_copy", [("PS", pi)], [("X", c)], X3[:, c, :], self.ps[pi][:, :N])

    def store_y(self, dst2d, N):
        P = self.P
        nb = (N + 127) // 128
        pr = min(N, 128)
        Ot = self.Or[:, :nb * D].rearrange("p (b d) -> p b d", d=D)
        okeys = [("O", c) for c in range(16)]
        X3 = self.v3(self.Xr, N, c=16)
        n = 0
        for b in range(nb):
            for c4 in range(4):
                pi = self.psum()
                for j in range(4):
                    c = c4 * 4 + j
                    P.op("pe", "transpose", [("X", c), "CST"], [("PS", pi)], self.ps[pi][:pr, j * 128:(j + 1) * 128],
                         X3[:, c, b * 128:b * 128 + pr], self.ident)
                if n % 2:
                    P.op("act", "activation", [("PS", pi)], okeys, Ot[:pr, b, c4 * 512:(c4 + 1) * 512],
                         self.ps[pi][:pr, :], AF.Copy)
                else:
                    P.op("dve", "tensor_copy", [("PS", pi)], okeys, Ot[:pr, b, c4 * 512:(c4 + 1) * 512],
                         self.ps[pi][:pr, :])
                n += 1
        P.dma("pool", self.och, okeys, [], dst2d.rearrange("(b p) d -> p b d", p=pr), Ot[:pr, :, :])

    def ffn(self, l, N):
        P = self.P
        H3 = self.v3(self.Hr, N, BF16, c=16)
        O3 = self.v3(self.Or, N, c=16)
        self.prenorm(N, self.g_fpre[:, l * 16:(l + 1) * 16])
        wg, wu, wd = self.wb["ffn_w_gate"][l], self.wb["ffn_w_up"][l], self.wb["ffn_w_down"][l]
        kg, ku, kd = ("WS", "ffn_w_gate", l), ("WS", "ffn_w_up", l), ("WS", "ffn_w_down", l)
        HC = FCH // 2
        G3 = self.v3(self.Gr, N, BF16, c=HC)
        for half in range(2):
            groups = [[[(kg, wg, (half * HC + j) * 128, 128)], [(ku, wu, (half * HC + j) * 128, 128)]] for j in range(HC)]

            def epi(gi, held):
                (pg, _), (pu, _) = held
                P.op("act", "activation", [("PS", pg)], ["T1"], self.T1[:, :N], self.ps[pg][:, :N], AF.Silu)
                P.op("dve", "tensor_tensor", ["T1", ("PS", pu)], [("G", gi)], G3[:, gi, :], self.T1[:, :N],
                     self.ps[pu][:, :N], op=ALU.mult)
            self.gemm(groups, 16, lambda k: H3[:, k, :], lambda k: ("H", k), N, epi, "ffn1")
            groups = [[[(kd, wd[half * HC * 128:(half + 1) * HC * 128, :], m * 128, 128)]] for m in range(16)]

            def epi2(gi, held, half=half):
                (pi, _), = held
                if half == 0:
                    if gi % 2:
                        P.op("act", "activation", [("PS", pi)], [("O", gi)], O3[:, gi, :], self.ps[pi][:, :N], AF.Copy)
                    else:
                        P.op("dve", "tensor_copy", [("PS", pi)], [("O", gi)], O3[:, gi, :], self.ps[pi][:, :N])
                else:
                    P.op("dve", "tensor_tensor", [("PS", pi), ("O", gi)], [("O", gi)], O3[:, gi, :], O3[:, gi, :],
                         self.ps[pi][:, :N], op=ALU.add)
            self.gemm(groups, HC, lambda k: G3[:, k, :], lambda k: ("G", k), N, epi2, "ffn2")
        self.postnorm_add(N, self.g_fpost[:, l * 16:(l + 1) * 16])

    def conformer(self, N, st):
        P = self.P
        H3 = self.v3(self.Hr, N, BF16, c=16)
        O3 = self.v3(self.Or, N, c=16)
        EW = 30 + N
        EXT = self.Gr[:, :16 * EW].rearrange("p (c n) -> p c n", n=EW)
        EXTH = self.EXTH.rearrange("p (c n) -> p c n", n=30)
        gk = [("G", c) for c in range(34)]
        P.op("dve", "tensor_copy", ["EXTH"], gk, EXT[:, :, 0:30], EXTH)
        w1 = self.wb["conv_a_w1"]
        k1 = ("WS", "conv_a_w1", 0)
        groups = [[[(k1, w1, c * 128, 128)], [(k1, w1, D + c * 128, 128)]] for c in range(16)]

        def epi(gi, held):
            (pa, _), (pg, _) = held
            P.op("act", "activation", [("PS", pa), "ca_b1"], ["T1"], self.T1[:, :N], self.ps[pa][:, :N], AF.Identity,
                 bias=self.ca_b1[:, gi:gi + 1])
            P.op("act", "activation", [("PS", pg), "ca_b1"], ["T2"], self.T2[:, :N], self.ps[pg][:, :N], AF.Sigmoid,
                 bias=self.ca_b1[:, 16 + gi:17 + gi])
            P.op("dve", "tensor_tensor", ["T1", "T2"], gk, EXT[:, gi, 30:30 + N], self.T1[:, :N], self.T2[:, :N],
                 op=ALU.mult)
        self.gemm(groups, 16, lambda k: H3[:, k, :], lambda k: ("H", k), N, epi, "ca1")
        dw = self.ca_dw.rearrange("p (j c) -> p j c", c=16)
        for c0 in range(0, 16, 4):
            for j in range(31):
                for c in range(c0, c0 + 4):
                    if j == 0:
                        P.op("dve", "tensor_scalar", gk + ["ca_dw", "ca_dwb"], [("O", c)], O3[:, c, :], EXT[:, c, 0:N],
                             dw[:, 0, c:c + 1], self.ca_dwb[:, c:c + 1], op0=ALU.mult, op1=ALU.add)
                    else:
                        P.op("dve", "scalar_tensor_tensor", gk + ["ca_dw", ("O", c)], [("O", c)], O3[:, c, :],
                             EXT[:, c, j:j + N], dw[:, j, c:c + 1], O3[:, c, :], op0=ALU.mult, op1=ALU.add)
        P.op("pool", "tensor_copy", gk, ["EXTH"], EXTH, EXT[:, :, N:N + 30])
        if st["last"]:
            self.out_rows(EXTH, "EXTH", 16, 30, st["o_ca"])
        CB = self.v3(self.Hr, N, BF16, c=16)
        CSQ = self.v3(self.Gr, N, BF16, c=16)
        for c in range(16):
            P.op("act", "activation", [("O", c)], [("H", c)], CB[:, c, :], O3[:, c, :], AF.Copy)
            P.op("act", "activation", [("O", c)], gk, CSQ[:, c, :], O3[:, c, :], AF.Square)
        pa, pb = self.psum(), self.psum()
        onesb = self.onesb.bitcast(BF16)
        for c in range(16):
            P.op("pe", "matmul", ["onesb", ("H", c)], [("PS", pa)], self.ps[pa][:, :N], onesb, CB[:, c, :],
                 start=(c == 0), stop=(c == 15))
        for c in range(16):
            P.op("pe", "matmul", ["onesb"] + gk, [("PS", pb)], self.ps[pb][:, :N], onesb, CSQ[:, c, :],
                 start=(c == 0), stop=(c == 15))
        MEAN, MSQ = self.T1[:, :N], self.T2[:, :N]
        P.op("act", "activation", [("PS", pa)], ["T1"], MEAN, self.ps[pa][:, :N], AF.Copy, scale=1.0 / D)
        P.op("dve", "tensor_tensor", ["T1"], ["T2"], MSQ, MEAN, MEAN, op=ALU.mult)
        P.op("dve", "scalar_tensor_tensor", [("PS", pb), "T2"], ["T2"], MSQ, self.ps[pb][:, :N], 1.0 / D, MSQ,
             op0=ALU.mult, op1=ALU.subtract)
        P.op("act", "activation", ["T2"], ["RS"], self.RS[:, :N], MSQ, AF.Sqrt, bias=self.epsc, scale=1.0)
        P.op("dve", "reciprocal", ["RS"], ["RS"], self.RS[:, :N], self.RS[:, :N])
        for c in range(16):
            P.op("dve", "tensor_tensor", [("O", c), "T1"], [("O", c)], O3[:, c, :], O3[:, c, :], MEAN, op=ALU.subtract)
            P.op("dve", "tensor_tensor", [("O", c), "RS"], [("O", c)], O3[:, c, :], O3[:, c, :], self.RS[:, :N],
                 op=ALU.mult)
            P.op("act", "activation", [("O", c), "ca_lng", "ca_lnb"], [("H", c)], H3[:, c, :], O3[:, c, :], AF.Silu,
                 bias=self.ca_lnb[:, c:c + 1], scale=self.ca_lng[:, c:c + 1])
        w2 = self.wb["conv_a_w2"]
        k2 = ("WS", "conv_a_w2", 0)
        groups = [[[(k2, w2, (2 * m + i) * 128, 128)] for i in range(2)] for m in range(8)]

        def epi2(gi, held):
            for i, (pi, _) in enumerate(held):
                m = 2 * gi + i
                P.op("act", "activation", [("PS", pi), "ca_b2"], [("O", m)], O3[:, m, :], self.ps[pi][:, :N],
                     AF.Identity, bias=self.ca_b2[:, m:m + 1])
        self.gemm(groups, 16, lambda k: H3[:, k, :], lambda k: ("H", k), N, epi2, "ca2")

    def sconv(self, N, st):
        P = self.P
        H3 = self.v3(self.Hr, N, BF16, c=16)
        G3 = self.v3(self.Gr, N, BF16, c=16)
        O3 = self.v3(self.Or, N, c=16)
        SCH = self.SCH.rearrange("p (j c) -> p j c", c=16)
        scw = self.sc_w.rearrange("p (j c) -> p j c", c=16)
        EW = 2 + N
        w = self.wb["sconv_w_in"]
        kk = ("WS", "sconv_w_in", 0)
        groups = [[[(kk, w, c * 128, 128)], [(kk, w, D + c * 128, 128)]] for c in range(16)]
        groups2 = [[[(kk, w, 2 * D + c * 128, 128)]] for c in range(16)]
        allg = []
        for c in range(16):
            allg.append(groups[c])
            allg.append(groups2[c])
        EXTa = self.Or[:, 0:EW]
        BGb = self.Or[:, 1024:1024 + N]
        CGb = self.Or[:, 2048:2048 + N]
        Yb = self.Or[:, 3072:3072 + N]

        def epi(gi, held):
            c = gi // 2
            if gi % 2 == 0:
                (pb, _), (pc, _) = held
                P.op("act", "activation", [("PS", pb)], ["sBG"], BGb, self.ps[pb][:, :N], AF.Copy)
                P.op("act", "activation", [("PS", pc)], ["sCG"], CGb, self.ps[pc][:, :N], AF.Copy)
            else:
                (px, _), = held
                P.op("dve", "tensor_copy", ["SCH"], ["sEXT"], EXTa[:, 0:2], SCH[:, :, c])
                P.op("dve", "tensor_tensor", ["sCG", ("PS", px)], ["sEXT"], EXTa[:, 2:2 + N], CGb, self.ps[px][:, :N],
                     op=ALU.mult)
                P.op("dve", "tensor_scalar", ["sEXT", "sc_w"], ["sY"], Yb, EXTa[:, 0:N], scw[:, 0, c:c + 1], None,
                     op0=ALU.mult)
                for j in (1, 2):
                    P.op("dve", "scalar_tensor_tensor", ["sEXT", "sc_w", "sY"], ["sY"], Yb, EXTa[:, j:j + N],
                         scw[:, j, c:c + 1], Yb, op0=ALU.mult, op1=ALU.add)
                P.op("dve", "tensor_copy", ["sEXT"], ["SCH"], SCH[:, :, c], EXTa[:, N:N + 2])
                P.op("dve", "tensor_tensor", ["sY", "sBG"], [("G", c)], G3[:, c, :], Yb, BGb, op=ALU.mult)
        self.gemm(allg, 16, lambda k: H3[:, k, :], lambda k: ("H", k), N, epi, "sc1")
        if st["last"]:
            self.out_rows(SCH, "SCH", 2, 16, st["o_sc"], jmajor=True)
        w2 = self.wb["sconv_w_out"]
        k2 = ("WS", "sconv_w_out", 0)
        groups = [[[(k2, w2, (2 * m + i) * 128, 128)] for i in range(2)] for m in range(8)]
        tmpk = ["sBG", "sCG", "sEXT", "sY"]

        def epi2(gi, held):
            for i, (pi, _) in enumerate(held):
                m = 2 * gi + i
                P.op("act", "activation", [("PS", pi)], [("O", m)] + tmpk, O3[:, m, :], self.ps[pi][:, :N], AF.Copy)
        self.gemm(groups, 16, lambda k: G3[:, k, :], lambda k: ("G", k), N, epi2, "sc2")


    def setup_swa(self):
        P = self.P
        kw = dict(allow_slow_non_contiguous=True)
        sk = self.sm["swa_sinks"][0:1, :].rearrange("o (c t) -> o c t", t=2)
        pairs = [(self.SINK[hh * 64:(hh + 1) * 64, :], sk[:, :, hh].broadcast_to([64, 16])) for hh in range(2)]
        P.dma_group("sp", self.mch, [], ["SINK"], pairs, **kw)
        OH = self.arena[:32, 0:12288]
        RB = self.arena[:32, 30000:30032]
        BT = self.arena[:32, 16384:16384 + 12288]
        P.dma("sp", self.mch, [], ["OHs"], OH, self.ohd)
        P.dma("sp", self.mch, [], ["RBs"], RB, self.sm["rel_bias"])
        for i in range(24):
            pi = self.psum()
            P.op("pe", "matmul", ["OHs", "RBs"], [("PS", pi)], self.ps[pi][:32, :], RB, OH[:, i * 512:(i + 1) * 512],
                 start=True, stop=True)
            P.op("dve", "tensor_copy", [("PS", pi)], ["BTs"], BT[:, i * 512:(i + 1) * 512], self.ps[pi][:32, :])
        P.dma("sp", self.mch, ["BTs"], ["BTD"], self.bt_d, BT)
        P.barrier([self.mch])

    def swa_seq_init(self, kind):
        P = self.P
        KTH = self.KTH.bitcast(BF16).rearrange("p (h n) -> p h n", n=128)
        VH = self.VH.bitcast(BF16)[:64, :].rearrange("p (b h d) -> p b h d", h=8, d=128)
        if kind == "p":
            P.op("pool", "memset", [], ["KTH"], self.KTH, 0.0)
            P.op("pool", "memset", [], ["VH"], self.VH, 0.0)
            return
        P.barrier([self.mch, self.och, self.ich])
        STK = self.Or[:, 0:1024].rearrange("p (h t d) -> p h t d", t=2, d=64)
        ck = self.c_k.rearrange("p (h d) -> p h d", d=64)
        P.dma_group("sp", self.mch, [], ["STK"], [(STK[:, :, t, :], ck) for t in range(2)])
        STV = self.Or[:64, 1024:2048].rearrange("p (b f) -> p b f", f=512)
        P.dma("sp", self.mch, [], ["STV"], STV, self.c_v.rearrange("(b p) f -> p b f", p=64))
        for n in range(8):
            pi = self.psum()
            P.op("pe", "transpose", ["STK", "CST"], [("PS", pi)], self.ps[pi][:, 0:128],
                 STK[:, n, :, :].rearrange("p t d -> p (t d)"), self.ident)
            P.op("act", "activation", [("PS", pi)], ["KTH"], KTH[:, n, :], self.ps[pi][:, 0:128], AF.Copy)
        for b in range(2):
            for t in range(2):
                P.op("dve", "tensor_copy", ["STV"], ["VH"], VH[:, b, :, t * 64:(t + 1) * 64],
                     STV[:, b, :].rearrange("p (h d) -> p h d", d=64))
        P.barrier([self.mch])

    def swa(self, N, st):
        P = self.P
        CH = N // 64
        P.barrier([self.mch, self.och, self.ich])
        G, O = self.Gr, self.Or
        KW = 128 + N
        KT = G[:, 0:2560].bitcast(BF16)[:, :8 * KW].rearrange("p (h n) -> p h n", n=KW)
        VD = G[:, 2560:7680].bitcast(BF16)[:64, :(CH + 2) * 1024].rearrange("p (b h d) -> p b h d", h=8, d=128)
        QB = [G[:, 7680 + 512 * i:7680 + 512 * (i + 1)].bitcast(BF16)[:, :CH * 128].rearrange("p (c m) -> p c m", m=128)
              for i in range(2)]
        AT = O[:, 0:4096].bitcast(BF16)[:, :16 * N].rearrange("p (c n) -> p c n", n=N)
        BI = O[:, 4096:7168].rearrange("p (c k) -> p c k", k=192)
        SS = [O[:, 7168 + 192 * i:7168 + 192 * (i + 1)] for i in range(4)]
        COL = O[:, 7936:7968]
        PP = [(self.T1 if i < 2 else self.T2)[:, (i % 2) * 96:(i % 2) * 96 + 96].bitcast(BF16) for i in range(4)]
        PT = [self.RS[:64, 0:192], self.RS[:64, 192:384], self.T1[:64, 256:448], self.T2[:64, 256:448]]
        PT = [p.bitcast(BF16).rearrange("p (b m) -> p b m", m=128) for p in PT]
        H3 = self.v3(self.Hr, N, BF16, c=16)
        O3 = self.v3(self.Or, N, c=16)
        KTH = self.KTH.bitcast(BF16).rearrange("p (h n) -> p h n", n=128)
        VH = self.VH.bitcast(BF16)[:64, :].rearrange("p (b h d) -> p b h d", h=8, d=128)
        identb = self.identb.bitcast(BF16)
        pairs = [(BI[hh * 64:(hh + 1) * 64, c, :], self.bt_d[2 * c + hh].rearrange("(q k) -> q k", k=192))
                 for c in range(16) for hh in range(2)]
        P.dma_group("sp", self.mch, ["BTD"], ["BI"], pairs)
        P.op("pool", "tensor_copy", ["KTH"], ["KT"], KT[:, :, 0:128], KTH)
        P.op("pool", "tensor_copy", ["VH"], ["VD"], VD[:, 0:2], VH)
        for i in range(2):
            P.op("pool", "memset", [], [("QB", i)], QB[i], 0.0)
        if self.cfg.get("dbg", 0) == 4:
            P.barrier([self.mch, self.och])
            self.zero_mixer(N)
            return
        w = self.wb["swa_w_qkv"]
        wk = ("WS", "swa_w_qkv", 0)
        groups = []
        for n in range(8):
            groups.append([[(wk, w, 2048 + n * 64, 64), (wk, w, 2048 + n * 64, 64)]])
        for n in range(8):
            groups.append([[(wk, w, 2560 + n * 64, 64), (wk, w, 2560 + n * 64, 64)]])
        for c in range(16):
            groups.append([[(wk, w, c * 128, 128)]])
        if self.cfg.get("dbg", 0) == 7:
            for n in range(8):
                groups[n] = [[(wk, w, 2048 + (n // 2) * 128, 128)]]
                groups[8 + n] = [[(wk, w, 2560 + (n // 2) * 128, 128)]]
        nnew = min(N, 128)
        first = st["first"]

        def attn(c, cis):
            n = c // 2
            qb = QB[c % 2]
            info = []
            for i, ci in enumerate(cis):
                base = ci * 64
                lo = max(base, 128) if first else base
                hi = base + 192
                info.append((i, ci, base, lo, hi - lo))
            sp = {}
            for (i, ci, base, lo, nk) in info:
                pi = self.psum()
                sp[i] = pi
                P.op("pe", "matmul", [("QB", c % 2), "KT"], [("PS", pi)], self.ps[pi][:, :nk], qb[:, ci, :],
                     KT[:, n, lo:lo + nk], start=True, stop=True)
            for (i, ci, base, lo, nk) in info:
                P.op("dve", "scalar_tensor_tensor", [("PS", sp[i]), "BI"], [("SS", i)], SS[i][:, :nk],
                     self.ps[sp[i]][:, :nk], 0.125, BI[:, c, lo - base:lo - base + nk], op0=ALU.mult, op1=ALU.add)
            for (i, ci, base, lo, nk) in info:
                P.op("dve", "tensor_reduce", [("SS", i)], [("MX", i)], COL[:, 4 * i:4 * i + 1], SS[i][:, :nk],
                     AX.X, ALU.max)
            for (i, ci, base, lo, nk) in info:
                P.op("dve", "tensor_scalar", [("MX", i), "SINK"], [("MX", i)], COL[:, 4 * i:4 * i + 1],
                     COL[:, 4 * i:4 * i + 1], self.SINK[:, c:c + 1], -1.0, op0=ALU.max, op1=ALU.mult)
            for (i, ci, base, lo, nk) in info:
                P.op("act", "activation", [("SS", i), ("MX", i)], [("SS", i), ("RSUM", i)], SS[i][:, :nk], SS[i][:, :nk],
                     AF.Exp, bias=COL[:, 4 * i:4 * i + 1], accum_out=COL[:, 4 * i + 1:4 * i + 2])
                P.op("act", "activation", ["SINK", ("MX", i)], [("ES", i)], COL[:, 4 * i + 2:4 * i + 3],
                     self.SINK[:, c:c + 1], AF.Exp, bias=COL[:, 4 * i:4 * i + 1])
            for (i, ci, base, lo, nk) in info:
                P.op("dve", "tensor_tensor", [("RSUM", i), ("ES", i)], [("RSUM", i)], COL[:, 4 * i + 1:4 * i + 2],
                     COL[:, 4 * i + 1:4 * i + 2], COL[:, 4 * i + 2:4 * i + 3], op=ALU.add)
                P.op("dve", "reciprocal", [("RSUM", i)], [("RSUM", i)], COL[:, 4 * i + 1:4 * i + 2],
                     COL[:, 4 * i + 1:4 * i + 2])
            for (i, ci, base, lo, nk) in info:
                P.op("dve", "tensor_scalar", [("SS", i), ("RSUM", i)], [("PB", i)], PP[i][:, :nk], SS[i][:, :nk],
                     COL[:, 4 * i + 1:4 * i + 2], None, op0=ALU.mult)
            if self.cfg.get("dbg", 0) == 3:
                return
            tp = {}
            for (i, ci, base, lo, nk) in info:
                pi = self.psum()
                tp[i] = pi
                pb = self.ps[pi][:].bitcast(BF16)
                for b in range(nk // 64):
                    P.op("pe", "transpose", [("PB", i), "identb"], [("PS", pi)], pb[:64, b * 128:(b + 1) * 128],
                         PP[i][:, b * 64:(b + 1) * 64], identb)
            for (i, ci, base, lo, nk) in info:
                nb = nk // 64
                pb = self.ps[tp[i]][:].bitcast(BF16)
                P.op("act", "activation", [("PS", tp[i])], [("PT", i)], PT[i][:, :nb, :],
                     pb[:64, :nb * 128].rearrange("p (b m) -> p b m", m=128), AF.Copy)
            op_ = {}
            for (i, ci, base, lo, nk) in info:
                pi = self.psum()
                op_[i] = pi
                nb = nk // 64
                for b in range(nb):
                    P.op("pe", "matmul", ["VD", ("PT", i)], [("PS", pi)], self.ps[pi][:, :128], VD[:, lo // 64 + b, n, :],
                         PT[i][:, b, :], start=(b == 0), stop=(b == nb - 1))
            for (i, ci, base, lo, nk) in info:
                pi = op_[i]
                P.op("act", "activation", [("PS", pi)], [("AT", c)], AT[0:64, c, ci * 64:(ci + 1) * 64],
                     self.ps[pi][0:64, 0:64], AF.Copy)
                P.op("dve", "tensor_copy", [("PS", pi)], [("AT", c)], AT[64:128, c, ci * 64:(ci + 1) * 64],
                     self.ps[pi][64:128, 64:128])

        dbg = self.cfg.get("dbg", 0)

        def epi(gi, held):
            (pi, _), = held
            if dbg == 8 and gi < 16:
                P.op("act", "activation", [("PS", pi)], ["KT"], KT[:, gi % 8, 128:128 + N], self.ps[pi][:, :N], AF.Copy)
                return
            if (dbg == 10 and 8 <= gi < 16) or (dbg == 11 and gi < 8) or (dbg in (10, 11) and gi >= 16):
                P.op("act", "activation", [("PS", pi)], ["KT"], KT[:, gi % 8, 128:128 + N], self.ps[pi][:, :N], AF.Copy)
                return
            if dbg == 9 and gi >= 16:
                P.op("act", "activation", [("PS", pi)], ["KT"], KT[:, gi % 8, 128:128 + N], self.ps[pi][:, :N], AF.Copy)
                return
            if gi < 8:
                n = gi
                P.op("act", "activation", [("PS", pi)], ["T1"], self.T1[:, :N], self.ps[pi][:, :N], AF.Copy)
                P.op("dve", "tensor_copy", ["T1"], ["KT"], KT[:, n, 128:128 + N], self.T1[:, :N])
                if st["last"]:
                    p2 = self.psum()
                    P.op("pe", "transpose", ["T1", "CST"], [("PS", p2)], self.ps[p2][:nnew, 0:128], self.T1[:, N - nnew:N],
                         self.ident)
                    P.op("dve", "tensor_copy", [("PS", p2)], ["STG"], self.STG[:nnew, n * 64:(n + 1) * 64],
                         self.ps[p2][:nnew, 0:64])
                    if n == 7:
                        P.dma("pool", self.och, ["STG"], [], st["o_k"][128 - nnew:128, :], self.STG[:nnew, :])
            elif gi < 16:
                n = gi - 8
                P.op("act", "activation", [("PS", pi)], ["T2"], self.T2[:, :N], self.ps[pi][:, :N], AF.Copy)
                for b0 in range(0, CH, 4):
                    nb = min(4, CH - b0)
                    p2 = self.psum()
                    for b in range(nb):
                        P.op("pe", "transpose", ["T2", "CST"], [("PS", p2)], self.ps[p2][:64, b * 128:(b + 1) * 128],
                             self.T2[:, (b0 + b) * 64:(b0 + b + 1) * 64], self.ident)
                    P.op("dve", "tensor_copy", [("PS", p2)], ["VD"], VD[:, 2 + b0:2 + b0 + nb, n, :],
                         self.ps[p2][:64, :nb * 128].rearrange("p (b m) -> p b m", m=128))
                if st["last"]:
                    p2 = self.psum()
                    P.op("pe", "transpose", ["T2", "CST"], [("PS", p2)], self.ps[p2][:nnew, 0:128],
                         self.T2[:, N - nnew:N], self.ident)
                    P.op("dve", "tensor_copy", [("PS", p2)], ["STG"], self.STG[:nnew, n * 64:(n + 1) * 64],
                         self.ps[p2][:nnew, 0:64])
                    if n == 7:
                        P.dma("pool", self.och, ["STG"], [], st["o_v"][128 - nnew:128, :], self.STG[:nnew, :])
            else:
                c = gi - 16
                qb = QB[c % 2]
                P.op("act", "activation", [("PS", pi)], [("QB", c % 2)], qb[0:64, :, 0:64],
                     self.ps[pi][0:64, :N].rearrange("p (c q) -> p c q", q=64), AF.Copy)
                P.op("dve", "tensor_copy", [("PS", pi)], [("QB", c % 2)], qb[64:128, :, 64:128],
                     self.ps[pi][64:128, :N].rearrange("p (c q) -> p c q", q=64))
                if self.cfg.get("dbg", 0) not in (2, 5, 7, 8, 9, 10, 11):
                    for g0 in range(0, CH, 4):
                        attn(c, list(range(g0, min(CH, g0 + 4))))
        self.gemm(groups, 16, lambda k: H3[:, k, :], lambda k: ("H", k), N, epi, "swa1")
        if self.cfg.get("dbg", 0) in (5, 7, 8, 9, 10, 11):
            P.barrier([self.mch, self.och])
            self.zero_mixer(N)
            return
        if st["last"] and nnew < 128:
            P.dma("pool", self.och, [], [], st["o_k"][0:128 - nnew, :], self.c_k[nnew:128, :])
            P.dma("pool", self.och, [], [], st["o_v"][0:128 - nnew, :], self.c_v[nnew:128, :])
        P.op("pool", "tensor_copy", ["KT"], ["KTH"], KTH, KT[:, :, N:N + 128])
        P.op("pool", "tensor_copy", ["VD"], ["VH"], VH, VD[:, CH:CH + 2])
        for c in range(16):
            P.op("dve" if c % 2 else "act", "tensor_copy" if c % 2 else "activation", [("AT", c)], [("H", c)],
                 H3[:, c, :], AT[:, c, :], *(() if c % 2 else (AF.Copy,)))
        P.barrier([self.mch, self.och])
        w2 = self.wb["swa_w_out"]
        k2 = ("WS", "swa_w_out", 0)
        groups = [[[(k2, w2, (2 * m + i) * 128, 128)] for i in range(2)] for m in range(8)]

        def epi2(gi, held):
            for i, (pi, _) in enumerate(held):
                m = 2 * gi + i
                P.op("act" if i else "dve", "activation" if i else "tensor_copy", [("PS", pi)], [("O", m)], O3[:, m, :],
                     self.ps[pi][:, :N], *((AF.Copy,) if i else ()))
        self.gemm(groups, 16, lambda k: H3[:, k, :], lambda k: ("H", k), N, epi2, "swa2")


    def setup_delta(self):
        P = self.P
        kw = dict(allow_slow_non_contiguous=True)
        DCW = self.DCW.rearrange("p (j c) -> p j c", c=64)
        cw = self.sm["delta_conv_w"][0]
        P.dma_group("sp", self.mch, [], ["DCW"], [(DCW[:, j, :], cw[j].rearrange("(c p) -> p c", p=128)) for j in range(4)], **kw)
        col = lambda ap: ap.rearrange("(p o) -> p o", o=1)
        P.dma("sp", self.mch, [], ["dcol"], self.dcol[32:64, 0:1], col(self.sm["delta_a_log"][0]), **kw)
        P.dma("sp", self.mch, [], ["dcol"], self.dcol[32:64, 1:2], col(self.sm["delta_dt_bias"][0]), **kw)
        P.dma("sp", self.mch, [], ["dcol"], self.dcol[:, 2:3], col(self.sm["delta_norm_g"][0]), **kw)
        P.op("act", "activation", ["dcol"], ["dcol"], self.dcol[32:64, 0:1], self.dcol[32:64, 0:1], AF.Exp)
        P.op("dve", "tensor_scalar", ["dcol"], ["dcol"], self.dcol[32:64, 0:1], self.dcol[32:64, 0:1], -1.0, None, op0=ALU.mult)
        P.op("pool", "memset", [], ["ones64"], self.ones64, 1.0)

    def delta_seq_init(self, kind):
        P = self.P
        kw = dict(allow_slow_non_contiguous=True)
        S3 = self.Sr.rearrange("p (h v) -> p h v", v=128)
        DCH = self.DCH.rearrange("p (j c) -> p j c", c=64)
        if kind == "p":
            P.op("pool", "memset", [], ["S"], self.Sr, 0.0)
            P.op("pool", "memset", [], ["DCH"], self.DCH, 0.0)
        else:
            P.dma("sp", self.mch, [], ["S"], S3, self.c_ds.rearrange("h k v -> k h v"))
            P.dma_group("sp", self.mch, [], ["DCH"], [(DCH[:, j, :], self.c_dc[j].rearrange("(c p) -> p c", p=128)) for j in range(3)], **kw)

    def psw(self):
        pi = self.psum()
        return ("PS", pi), self.ps[pi][:, 0:128]

    def psq(self, eng="dve"):
        if eng == "act":
            i = self.psqa % 8
            self.psqa += 1
            b, q = 4 + i // 4, i % 4
        else:
            i = self.psqn % 8
            self.psqn += 1
            b, q = 6 + i // 4, i % 4
        return ("PSQ", b, q), self.ps[b][:, q * 128:(q + 1) * 128]

    def delta(self, N, st):
        try:
            self.delta_(N, st)
        except StopIteration:
            self.P.barrier([self.mch, self.och])
            self.ps_lo, self.ps_n = 0, 8
            self.zero_mixer(N)

    def dstop(self, lvl):
        if self.cfg.get("dbg", 0) == lvl:
            raise StopIteration

    def delta_(self, N, st):
        P = self.P
        L = min(N, 128)
        NB = L // 64
        U = N // L
        KSQ = 5
        P.barrier([self.mch, self.och, self.ich])
        self.ps_lo, self.ps_n = 0, 4
        self.psqn = 0
        self.psqa = 0
        O = self.Or
        pos = [0]

        def tk(n):
            a = pos[0]
            pos[0] += n
            assert pos[0] <= 8192, pos[0]
            return O[:, a:a + n]
        BG = tk(512)
        TMPG = tk(512)
        BGc = tk(64 * max(U, 4)).rearrange("p (u c) -> p u c", c=64)
        EXTd = tk(520)
        QF = tk(512)
        QN = tk(256).bitcast(BF16)
        KN = tk(256).bitcast(BF16)
        VT = [tk(256).bitcast(BF16) for _ in range(2)]
        ZS = [tk(256).bitcast(BF16) for _ in range(2)]
        RgS = tk(512)
        EgR = tk(512)
        QG = tk(256).bitcast(BF16)
        OTS = tk(512)
        NCHN = 2
        CHB = []
        for i in range(NCHN):
            d = {}
            d["A"] = tk(128)
            for nm in ("P0", "P1", "T0", "T1", "Q0", "Q1", "QKm", "BV", "BEK", "KG", "WT", "VN"):
                d[nm] = tk(64).bitcast(BF16)
            d["U"] = tk(128)
            d["col"] = tk(4)
            CHB.append(d)
        CV = self.T1
        TMPB = self.T2
        G3 = self.v3(self.Gr, N, BF16, c=32)
        H3 = self.v3(self.Hr, N, BF16, c=16)
        O3 = self.v3(self.Or, N, c=16)
        S3 = self.Sr.rearrange("p (h v) -> p h v", v=128)
        SBF = self.Gr[:, 8192:8256].bitcast(BF16)
        DCH = self.DCH.rearrange("p (j c) -> p j c", c=64)
        DCW = self.DCW.rearrange("p (j c) -> p j c", c=64)
        identb = self.identb.bitcast(BF16)
        onesb = self.onesb.bitcast(BF16)
        w = self.wb["delta_w_in"]
        wk = ("WS", "delta_w_in", 0)

        def epiA(gi, held):
            (pi, _), = held
            P.op("act", "activation", [("PS", pi)], ["BG"], BG[0:32, :N], self.ps[pi][0:32, :N], AF.Sigmoid)
            P.op("act", "activation", [("PS", pi), "dcol"], ["TMPG"], TMPG[32:64, :N], self.ps[pi][32:64, :N], AF.Exp,
                 bias=self.dcol[32:64, 1:2])
            P.op("act", "activation", ["TMPG"], ["TMPG"], TMPG[32:64, :N], TMPG[32:64, :N], AF.Ln, bias=self.onec[32:64, 0:1])
            P.op("dve", "tensor_scalar", ["TMPG", "dcol"], ["TMPG"], TMPG[32:64, :N], TMPG[32:64, :N],
                 self.dcol[32:64, 0:1], None, op0=ALU.mult)
            P.op("dve", "tensor_tensor_scan", ["TMPG", "CST"], ["BG"], BG[32:64, :N], self.reset[32:64, :N],
                 TMPG[32:64, :N], 0.0, op0=ALU.mult, op1=ALU.add)
            for u in range(U):
                kq, pq = self.psq()
                P.op("pe", "transpose", ["BG", "CST"], [kq], pq[:L, 0:64], BG[0:64, u * L:(u + 1) * L], self.ident[:64, :64])
                P.op("dve", "tensor_copy", [kq], ["BGc"], BGc[:L, u, :], pq[:L, 0:64])
        self.gemm([[[(wk, w, 12288, 64)]]], 16, lambda k: H3[:, k, :], lambda k: ("H", k), N, epiA, "dA")
        self.dstop(21)

        def conv_chunk(pi, chn, dst_kind, idx):
            P.op("dve", "tensor_copy", ["DCH"], ["EXTd"], EXTd[:, 0:3], DCH[:, :, chn])
            P.op("act", "activation", [("PS", pi)], ["EXTd"], EXTd[:, 3:3 + N], self.ps[pi][:, :N], AF.Copy)
            P.op("dve", "tensor_copy", ["EXTd"], ["DCH"], DCH[:, :, chn], EXTd[:, N:N + 3])
            P.op("dve", "tensor_scalar", ["EXTd", "DCW"], ["CV"], CV[:, :N], EXTd[:, 0:N], DCW[:, 0, chn:chn + 1], None,
                 op0=ALU.mult)
            for j in (1, 2, 3):
                P.op("dve", "scalar_tensor_tensor", ["EXTd", "DCW", "CV"], ["CV"], CV[:, :N], EXTd[:, j:j + N],
                     DCW[:, j, chn:chn + 1], CV[:, :N], op0=ALU.mult, op1=ALU.add)
            if dst_kind == "v":
                P.op("act", "activation", ["CV"], [("VT", idx)], VT[idx][:, :N], CV[:, :N], AF.Silu)
            else:
                P.op("act", "activation", ["CV"], ["QF"], QF[:, :N], CV[:, :N], AF.Silu)
                P.op("act", "activation", ["QF"], ["TMPB"], TMPB.bitcast(BF16)[:, :N], QF[:, :N], AF.Square)
                p2 = self.psum()
                P.op("pe", "matmul", ["onesb", "TMPB"], [("PS", p2)], self.ps[p2][:, :N], onesb, TMPB.bitcast(BF16)[:, :N],
                     start=True, stop=True)
                P.op("act", "activation", [("PS", p2)], ["RS"], self.RS[:, :N], self.ps[p2][:, :N], AF.Sqrt,
                     bias=self.epsc, scale=1.0)
                P.op("dve", "reciprocal", ["RS"], ["RS"], self.RS[:, :N], self.RS[:, :N])
                dst = QN if dst_kind == "q" else KN
                sc = 128.0 ** -0.5 if dst_kind == "q" else 1.0
                P.op("dve", "scalar_tensor_tensor", ["QF", "RS"], [dst_kind + "N"], dst[:, :N], QF[:, :N], sc, self.RS[:, :N],
                     op0=ALU.mult, op1=ALU.mult)

        def head(hq, hh):
            hv = 2 * hq + hh
            P.op("dve", "tensor_scalar", ["BG", "CST"], ["TMPB"], TMPB[0:64, :N], BG[0:64, :N],
                 self.ident[0:64, 32 + hv:33 + hv], None, op0=ALU.mult)
            p2 = self.psum()
            P.op("pe", "matmul", ["ones64", "TMPB"], [("PS", p2)], self.ps[p2][:, :N], self.ones64[0:64, :], TMPB[0:64, :N],
                 start=True, stop=True)
            P.op("act", "activation", [("PS", p2)], ["RgS"], RgS[:, :N], self.ps[p2][:, :N], AF.Copy)
            P.op("act", "activation", [("PS", p2)], ["EgR"], EgR[:, :N], self.ps[p2][:, :N], AF.Exp)
            P.op("dve", "tensor_tensor", ["qN", "EgR"], ["QG"], QG[:, :N], QN[:, :N], EgR[:, :N], op=ALU.mult)
            P.op("act", "activation", [("S", hv)], ["SBF"], SBF, S3[:, hv, :], AF.Copy)
            self.dstop(23)
            for u0 in range(0, U, NCHN):
                us = list(range(u0, min(U, u0 + NCHN)))
                chains(hq, hv, hh, us)
                self.dstop(25)
                for u in us:
                    recur(hv, u, u - u0)
            P.op("act", "activation", ["OTS"], ["TMPB"], TMPB.bitcast(BF16)[:, :N], OTS[:, :N], AF.Square)
            p2 = self.psum()
            P.op("pe", "matmul", ["onesb", "TMPB"], [("PS", p2)], self.ps[p2][:, :N], onesb, TMPB.bitcast(BF16)[:, :N],
                 start=True, stop=True)
            P.op("act", "activation", [("PS", p2)], ["RS"], self.RS[:, :N], self.ps[p2][:, :N], AF.Sqrt,
                 bias=self.epsc, scale=1.0 / 128)
            P.op("dve", "reciprocal", ["RS"], ["RS"], self.RS[:, :N], self.RS[:, :N])
            P.op("dve", "scalar_tensor_tensor", ["OTS", "RS", "dcol"], ["OTS"], OTS[:, :N], OTS[:, :N], self.dcol[:, 2:3],
                 self.RS[:, :N], op0=ALU.mult, op1=ALU.mult)
            P.op("dve", "tensor_tensor", ["OTS", ("ZS", hh)], [("G", hv)], G3[:, hv, :], OTS[:, :N], ZS[hh][:, :N], op=ALU.mult)

        def chains(hq, hv, hh, us):
            n = len(us)
            C = [CHB[i] for i in range(n)]
            cs = lambda u: slice(u * L, (u + 1) * L)
            kk, qk, ktk = {}, {}, {}
            for i, u in enumerate(us):
                kk[i] = self.psq()
                P.op("pe", "matmul", ["kN"], [kk[i][0]], kk[i][1][:L, :L], KN[:, cs(u)], KN[:, cs(u)], start=True, stop=True)
                qk[i] = self.psq()
                P.op("pe", "matmul", ["kN", "qN"], [qk[i][0]], qk[i][1][:L, :L], KN[:, cs(u)], QN[:, cs(u)], start=True, stop=True)
            for i, u in enumerate(us):
                gcc = BGc[:L, u, 32 + hv:33 + hv]
                P.op("dve", "tensor_scalar", ["RgS", "BGc"], [("A", i)], C[i]["A"][:L, :L], RgS[:L, cs(u)], gcc, 0.0,
                     op0=ALU.subtract, op1=ALU.max)
                P.op("act", "activation", [("A", i)], [("A", i)], C[i]["A"][:L, :L], C[i]["A"][:L, :L], AF.Exp, scale=-1.0)
                P.op("pool", "tensor_tensor", [("A", i), "CST"], [("A", i)], C[i]["A"][:L, :L], C[i]["A"][:L, :L],
                     self.maskLs[:L, :L], op=ALU.mult)
                P.op("dve", "scalar_tensor_tensor", [kk[i][0], "BGc", ("A", i)], [("P", i, 0)], C[i]["P0"][:L, :L],
                     kk[i][1][:L, :L], BGc[:L, u, hv:hv + 1], C[i]["A"][:L, :L], op0=ALU.mult, op1=ALU.mult)
            for i, u in enumerate(us):
                gcc = BGc[:L, u, 32 + hv:33 + hv]
                P.op("dve", "tensor_scalar", ["RgS", "BGc", ("P", i, 0)], [("A", i)], C[i]["A"][:L, :L], RgS[:L, cs(u)], gcc, 0.0,
                     op0=ALU.subtract, op1=ALU.min)
                P.op("act", "activation", [("A", i)], [("A", i)], C[i]["A"][:L, :L], C[i]["A"][:L, :L], AF.Exp)
                P.op("pool", "tensor_tensor", [("A", i), "CST"], [("A", i)], C[i]["A"][:L, :L], C[i]["A"][:L, :L],
                     self.maskU[:L, :L], op=ALU.mult)
                P.op("dve", "tensor_tensor", [qk[i][0], ("A", i)], [("QKm", i)], C[i]["QKm"][:L, :L], qk[i][1][:L, :L],
                     C[i]["A"][:L, :L], op=ALU.mult)
                colt = C[i]["col"]
                P.op("act", "activation", ["BGc"], [("col", i)], colt[:L, 0:1], gcc, AF.Exp)
                P.op("dve", "tensor_tensor", [("col", i), "BGc"], [("col", i)], colt[:L, 0:1], colt[:L, 0:1],
                     BGc[:L, u, hv:hv + 1], op=ALU.mult)
                for bb in range(NB):
                    rw = slice(bb * 64, (bb + 1) * 64)
                    lc = u * L + bb * 64 + 63
                    P.op("act", "activation", ["BGc", "RgS"], [("col", i)], colt[rw, 1:2], BGc[rw, u, 32 + hv:33 + hv], AF.Exp,
                         scale=-1.0, bias=RgS[rw, lc:lc + 1])
            self.dstop(24)
            tq = {}
            for i, u in enumerate(us):
                tq[i] = self.psq("act")
                pb = tq[i][1].bitcast(BF16)
                P.op("pe", "transpose", [("P", i, 0), "identb"], [tq[i][0]], pb[:L, 0:L], C[i]["P0"][:L, :L], identb[:L, :L])
            for i, u in enumerate(us):
                pb = tq[i][1].bitcast(BF16)
                P.op("act", "activation", [tq[i][0]], [("T", i, 0)], C[i]["T0"][:L, :L], pb[:L, 0:L], AF.Copy)
                P.op("dve", "tensor_tensor", [("T", i, 0), "identb"], [("Q", i, 0)], C[i]["Q0"][:L, :L], C[i]["T0"][:L, :L],
                     identb[:L, :L], op=ALU.add)
            self.dstop(27)
            kq, vq = {}, {}
            for i, u in enumerate(us):
                kq[i] = self.psq()
                P.op("pe", "transpose", ["kN", "identb"], [kq[i][0]], kq[i][1].bitcast(BF16)[:L, 0:128], KN[:, cs(u)], identb)
                vq[i] = self.psq()
                P.op("pe", "transpose", [("VT", hh), "identb"], [vq[i][0]], vq[i][1].bitcast(BF16)[:L, 0:128], VT[hh][:, cs(u)], identb)
            for i, u in enumerate(us):
                colt = C[i]["col"]
                P.op("dve", "tensor_scalar", [vq[i][0], "BGc"], [("BV", i)], C[i]["BV"][:L, :], vq[i][1].bitcast(BF16)[:L, 0:128],
                     BGc[:L, u, hv:hv + 1], None, op0=ALU.mult)
                P.op("dve", "tensor_scalar", [kq[i][0], ("col", i)], [("BEK", i)], C[i]["BEK"][:L, :], kq[i][1].bitcast(BF16)[:L, 0:128],
                     colt[:L, 0:1], None, op0=ALU.mult)
                P.op("dve", "tensor_scalar", [kq[i][0], ("col", i)], [("KG", i)], C[i]["KG"][:L, :], kq[i][1].bitcast(BF16)[:L, 0:128],
                     colt[:L, 1:2], None, op0=ALU.mult)
            self.dstop(28)
            for k in range(KSQ):
                a, b = k % 2, (k + 1) % 2
                Pa, Ta, Pb, Tb = "P%d" % a, "T%d" % a, "P%d" % b, "T%d" % b
                Qa, Qb = "Q%d" % a, "Q%d" % b
                r1, r2 = {}, {}
                for i in range(n):
                    r1[i] = self.psq("act") if self.cfg.get("dbg", 0) != 33 else self.psw()
                    P.op("pe", "matmul", [("P", i, a), ("T", i, a)], [r1[i][0]], r1[i][1][:L, :L], C[i][Ta][:L, :L], C[i][Pa][:L, :L],
                         start=True, stop=True)
                    if k < KSQ - 1:
                        r2[i] = self.psq() if self.cfg.get("dbg", 0) != 33 else self.psw()
                        P.op("pe", "matmul", [("P", i, a), ("T", i, a)], [r2[i][0]], r2[i][1][:L, :L], C[i][Pa][:L, :L],
                             C[i][Ta][:L, :L], start=True, stop=True)
                self.dstop(30)
                self.dstop(33)
                for i in range(n):
                    P.op("act", "activation", [r1[i][0]], [("P", i, b)], C[i][Pb][:L, :L], r1[i][1][:L, :L], AF.Copy)
                    if k < KSQ - 1:
                        P.op("dve", "tensor_copy", [r2[i][0]], [("T", i, b)], C[i][Tb][:L, :L], r2[i][1][:L, :L])
                self.dstop(31)
                r3 = {}
                for i in range(n):
                    r3[i] = self.psq()
                    P.op("pe", "matmul", [("P", i, b), ("Q", i, a)], [r3[i][0]], r3[i][1][:L, :L], C[i][Pb][:L, :L], C[i][Qa][:L, :L],
                         start=True, stop=True)
                for i in range(n):
                    P.op("dve", "tensor_tensor", [r3[i][0], ("Q", i, a)], [("Q", i, b)], C[i][Qb][:L, :L], r3[i][1][:L, :L],
                         C[i][Qa][:L, :L], op=ALU.add)
                self.dstop(32)
            self.dstop(29)
            qf = KSQ % 2
            Qf = "Q%d" % qf
            uq, wq = {}, {}
            for i in range(n):
                uq[i] = self.psq("act")
                P.op("pe", "matmul", [("Q", i, qf), ("BV", i)], [uq[i][0]], uq[i][1][:L, :], C[i][Qf][:L, :L], C[i]["BV"][:L, :],
                     start=True, stop=True)
                wq[i] = self.psq()
                P.op("pe", "matmul", [("Q", i, qf), ("BEK", i)], [wq[i][0]], wq[i][1][:, :L], C[i]["BEK"][:L, :], C[i][Qf][:L, :L],
                     start=True, stop=True)
            for i in range(n):
                P.op("act", "activation", [uq[i][0]], [("U", i)], C[i]["U"][:L, :], uq[i][1][:L, :], AF.Copy)
                P.op("dve", "tensor_copy", [wq[i][0]], [("WT", i)], C[i]["WT"][:, :L], wq[i][1][:, :L])

        def recur(hv, u, i):
            Cc = CHB[i]
            for bb in range(NB):
                rw = slice(bb * 64, (bb + 1) * 64)
                cb = slice(u * L + bb * 64, u * L + (bb + 1) * 64)
                lc = u * L + bb * 64 + 63
                k1, p1 = self.psq()
                P.op("pe", "matmul", [("WT", i), "SBF"], [k1], p1[:L, :], Cc["WT"][:, :L], SBF, start=True, stop=True)
                P.op("dve", "tensor_tensor", [("U", i), k1], [("VN", i)], Cc["VN"][rw, :], Cc["U"][rw, :], p1[rw, :], op=ALU.subtract)
                k2, p2 = self.psq("act")
                P.op("pe", "matmul", ["SBF", "QG"], [k2], p2[:, :64], SBF, QG[:, cb], start=True, stop=False)
                P.op("pe", "matmul", [("VN", i), ("QKm", i)], [k2], p2[:, :64], Cc["VN"][rw, :], Cc["QKm"][rw, bb * 64:(bb + 1) * 64],
                     start=False, stop=True)
                k3, p3 = self.psq()
                P.op("pe", "matmul", [("KG", i), ("VN", i)], [k3], p3[:, :], Cc["KG"][rw, :], Cc["VN"][rw, :], start=True, stop=True)
                P.op("dve", "scalar_tensor_tensor", [("S", hv), "EgR", k3], [("S", hv)], S3[:, hv, :], S3[:, hv, :],
                     EgR[:, lc:lc + 1], p3[:, :], op0=ALU.mult, op1=ALU.add)
                P.op("act", "activation", [("S", hv)], ["SBF"], SBF, S3[:, hv, :], AF.Copy)
                P.op("act", "activation", [k2], ["OTS"], OTS[:, cb], p2[:, :64], AF.Copy)

        for hq in range(16):
            groups = [
                [[(wk, w, hq * 128, 128)], [(wk, w, 2048 + hq * 128, 128)]],
                [[(wk, w, 4096 + (2 * hq) * 128, 128)], [(wk, w, 4096 + (2 * hq + 1) * 128, 128)]],
                [[(wk, w, 8192 + (2 * hq) * 128, 128)], [(wk, w, 8192 + (2 * hq + 1) * 128, 128)]],
            ]

            def epi(gi, held, hq=hq):
                if gi == 0:
                    conv_chunk(held[0][0], hq, "q", 0)
                    conv_chunk(held[1][0], 16 + hq, "k", 0)
                elif gi == 1:
                    conv_chunk(held[0][0], 32 + 2 * hq, "v", 0)
                    conv_chunk(held[1][0], 32 + 2 * hq + 1, "v", 1)
                else:
                    for i in range(2):
                        P.op("act", "activation", [("PS", held[i][0])], [("ZS", i)], ZS[i][:, :N], self.ps[held[i][0]][:, :N], AF.Silu)
            self.gemm(groups, 16, lambda k: H3[:, k, :], lambda k: ("H", k), N, epi, "dB")
            self.dstop(22)
            for hh in range(2):
                head(hq, hh)
                self.dstop(26)
        if st["last"]:
            self.out_rows(DCH, "DCH", 3, 64, st["o_dc"], jmajor=True)
            P.dma("pool", self.och, [("S", h) for h in range(32)], [], st["o_ds"].rearrange("h k v -> k h v"), S3)
        P.barrier([self.mch, self.och])
        self.ps_lo, self.ps_n = 0, 8
        w2 = self.wb["delta_w_out"]
        k2 = ("WS", "delta_w_out", 0)
        groups = [[[(k2, w2, m * 128, 128)]] for m in range(16)]

        def epi2(gi, held):
            (pi, _), = held
            P.op("act" if gi % 2 else "dve", "activation" if gi % 2 else "tensor_copy", [("PS", pi)], [("O", gi)], O3[:, gi, :],
                 self.ps[pi][:, :N], *((AF.Copy,) if gi % 2 else ()))
        self.gemm(groups, 32, lambda k: G3[:, k, :], lambda k: ("G", k), N, epi2, "dC")

    def out_rows(self, src3, key, A, Bn, dst, jmajor=False):
        P = self.P
        if not jmajor:
            nchunk, nrow = A, Bn
            for c4 in range(0, nchunk, 4):
                pi = self.psum()
                for j in range(4):
                    c = c4 + j
                    P.op("pe", "transpose", [key, "CST"], [("PS", pi)], self.ps[pi][:nrow, j * 128:(j + 1) * 128],
                         src3[:, c, :], self.ident)
                STG = self.STG[:nrow, :]
                P.op("dve", "tensor_copy", [("PS", pi)], ["STG"], STG, self.ps[pi][:nrow, :])
                P.dma("pool", self.och, ["STG"], [], dst[:, c4 * 128:(c4 + 4) * 128], STG)
        else:
            nrow, nchunk = A, Bn
            for j in range(nrow):
                pi = self.psum()
                P.op("pe", "transpose", [key, "CST"], [("PS", pi)], self.ps[pi][:nchunk, 0:128], src3[:, j, :], self.ident)
                STG = self.STG[:nchunk, j * 128:(j + 1) * 128]
                P.op("dve", "tensor_copy", [("PS", pi)], ["STG"], STG, self.ps[pi][:nchunk, 0:128])
                P.dma("pool", self.och, ["STG"], [], dst[j:j + 1, :].rearrange("o (c p) -> (o c) p", p=128), STG)

    def run_seq(self, kind, si):
        P = self.P
        cfg = self.cfg
        nl = cfg["nlayers"]
        if kind == "p":
            N = 512
            ntile = cfg["ptiles"]
        else:
            N = 64
            ntile = 1
        EXTH = self.EXTH.rearrange("p (c n) -> p c n", n=30)
        SCH = self.SCH.rearrange("p (j c) -> p j c", c=16)
        kw = dict(allow_slow_non_contiguous=True)
        if kind == "p":
            P.op("pool", "memset", [], ["EXTH"], self.EXTH, 0.0)
            P.op("pool", "memset", [], ["SCH"], self.SCH, 0.0)
        else:
            P.dma_group("sp", self.mch, [], ["EXTH"], [(EXTH[:, c, :], self.c_conv_a[:, c * 128:(c + 1) * 128].rearrange("r p -> p r")) for c in range(16)], **kw)
            P.dma_group("sp", self.mch, [], ["SCH"], [(SCH[:, j, :], self.c_sc[j].rearrange("(c p) -> p c", p=128)) for j in range(2)], **kw)
        if nl > 3:
            self.swa_seq_init(kind)
        if nl > 1:
            self.delta_seq_init(kind)
        for t in range(ntile):
            last = (t == ntile - 1) and (kind == "s" or ntile == 4)
            if kind == "p":
                src = self.xp[si, t * 512:(t + 1) * 512, :]
                dst = self.o_y_p[si, t * 512:(t + 1) * 512, :]
                st = {"last": last, "o_ca": self.o_ca_p[si], "o_sc": self.o_sc_p[si], "first": t == 0,
                      "o_k": self.o_k_p[si], "o_v": self.o_v_p[si],
                      "o_dc": self.o_dc_p[si], "o_ds": self.o_ds_p[si]}
            else:
                src = self.xs
                dst = self.o_y_s
                st = {"last": last, "o_ca": self.o_ca_s, "o_sc": self.o_sc_s, "first": False,
                      "o_k": self.o_k_s, "o_v": self.o_v_s, "o_dc": self.o_dc_s, "o_ds": self.o_ds_s}
            self.load_x(src, N)
            for l in range(nl):
                self.prenorm(N, self.g_mpre[:, l * 16:(l + 1) * 16])
                if l in cfg.get("skip", ()):
                    self.zero_mixer(N)
                elif l == 0:
                    self.conformer(N, st)
                elif l == 2:
                    self.sconv(N, st)
                elif l == 3:
                    if cfg.get("dbg", 0) == 1:
                        self.zero_mixer(N)
                    else:
                        self.swa(N, st)
                elif l == 1:
                    self.delta(N, st)
                else:
                    self.zero_mixer(N)
                self.postnorm_add(N, self.g_mpost[:, l * 16:(l + 1) * 16])
                self.ffn(l, N)
            self.store_y(dst, N)

    def zero_mixer(self, N):
        O3 = self.v3(self.Or, N, c=16)
        for c in range(16):
            self.P.op("pool", "memset", [], [("O", c)], O3[:, c, :], 1.0)

    def build(self):
        nc = self.nc
        P = self.P
        self.decl()
        with contextlib.ExitStack() as es:
            self.alloc(es)
            self.epsc = self.take(1)
            P.op("pool", "memset", [], ["eps"], self.epsc, EPS)
            P.op("pool", "memset", [], ["onec"], self.onec, 1.0)
            self.setup()
            if self.cfg["nlayers"] > 3:
                self.setup_swa()
            if self.cfg["nlayers"] > 1:
                self.setup_delta()
            for s in self.cfg["seqs"]:
                if s[0] == "p":
                    self.run_seq("p", int(s[1]))
                else:
                    self.run_seq("s", 0)
            for e in ("sp", "pool", "act"):
                P.final_waits(e)
            semh = {}
            for e in ("pe", "act", "dve", "pool"):
                semh[e] = es.enter_context(nc.semaphore("s_" + e))
            for ch in P.chans:
                if ch.count > 0:
                    semh[("ch", ch.idx)] = es.enter_context(nc.semaphore(f"c{ch.idx}"))
            block = es.enter_context(nc.Block())

            @block.tensor
            def _(e):
                P.replay("pe", e, semh)

            @block.scalar
            def _(e):
                P.replay("act", e, semh)

            @block.vector
            def _(e):
                P.replay("dve", e, semh)

            @block.gpsimd
            def _(e):
                P.replay("pool", e, semh)

            @block.sync
            def _(e):
                P.replay("sp", e, semh)
        return nc


def make_inputs(inputs, core):
    f = lambda a: np.ascontiguousarray(np.asarray(a, dtype=np.float32))
    m = {}
    m["x_prompt"] = f(inputs["x_prompt"][2 * core:2 * core + 2])
    m["x_sample"] = f(inputs["x_sample"][core])
    m["cache_conv_a"] = f(inputs["cache_conv_a"][0, core])
    m["state_delta_s"] = f(inputs["state_delta_s"][0, core])
    m["state_delta_conv"] = f(inputs["state_delta_conv"][0, core])
    m["cache_sconv"] = f(inputs["cache_sconv"][0, core])
    m["cache_swa_k"] = f(inputs["cache_swa_k"][0, core]).reshape(128, 512)
    m["cache_swa_v"] = f(inputs["cache_swa_v"][0, core]).reshape(128, 512)
    return m


_CACHE = {}


def run(inputs, cfg, trace=False, ncores=8):
    key = repr(sorted(cfg.items()))
    if key not in _CACHE:
        _CACHE[key] = K(cfg).build()
    nc = _CACHE[key]
    shared = {}
    for n, _ in SMALL:
        shared[n] = np.ascontiguousarray(np.asarray(inputs[n], dtype=np.float32))
    nl = cfg["nlayers"]
    for li, (n, _, _) in enumerate(WSPEC):
        if li // 2 < nl:
            shared[n] = np.ascontiguousarray(np.asarray(inputs[n], dtype=np.float32)[0])
    for n, _, _ in FFN_W:
        shared[n] = np.ascontiguousarray(np.asarray(inputs[n], dtype=np.float32)[:nl])
    shared["cst"] = host_consts(cfg.get("L", 64))
    shared["onehot"] = host_onehot()
    in_maps = []
    for c in range(ncores):
        m = dict(shared)
        m.update(make_inputs(inputs, c))
        in_maps.append(m)
    res = run_bass_kernel_spmd(nc, in_maps, core_ids=list(range(ncores)), trace=trace)
    return res


def kernel(**inputs):
    res = run(inputs, CFG)
    r = res.results
    cat = lambda n: np.concatenate([r[c][n] for c in range(8)], axis=0)
    stk = lambda n: np.stack([r[c][n] for c in range(8)], axis=0)
    y_p = cat("y_prompt")
    y_s = stk("y_sample")
    outs = (
        y_p, y_s,
        cat("conv_a_p")[None], stk("conv_a_s")[None],
        cat("ds_p")[None], stk("ds_s")[None],
        cat("dc_p")[None], stk("dc_s")[None],
        cat("sc_p")[None], stk("sc_s")[None],
        cat("k_p").reshape(16, 128, 8, 64)[None], stk("k_s").reshape(8, 128, 8, 64)[None],
        cat("v_p").reshape(16, 128, 8, 64)[None], stk("v_s").reshape(8, 128, 8, 64)[None],
    )
    return tuple(np.ascontiguousarray(o.astype(np.float32)) for o in outs)
```
